# Optimizing a Trainium2 kernel written in Bass

```python
import jax, jax.numpy as jnp
from jax import lax
import numpy as np

D_MODEL = 1024
BATCH = 8
SEQ = 4096
DEPTH = 1

CHUNK = 64
MIX_WIDTH = D_MODEL
CONV_CH = MIX_WIDTH // 2
CONV_WIDTH = 31
CONV_GROUPS = 8
HGRN_WIDTH = MIX_WIDTH - CONV_CH
HGRN_HEADS = 4
HGRN_DK = HGRN_WIDTH // HGRN_HEADS
HGRN_DV = HGRN_WIDTH // HGRN_HEADS
IN_COLS = 2 * CONV_CH + 4 * HGRN_WIDTH
D_FF = 4 * D_MODEL
RMS_EPS = 1e-6
GN_EPS = 1e-5

kernel_name = "hymba_conformer_hgrn2_adaln_block"


def _rms(x, w):
    xf = x.astype(jnp.float32)
    y = xf * lax.rsqrt(jnp.mean(xf * xf, axis=-1, keepdims=True) + RMS_EPS)
    return (y * w.astype(jnp.float32)).astype(x.dtype)


def _conv_mixer(u_val, u_gate, w_dw, b_dw, gn_gain, gn_bias):
    B, S, _ = u_val.shape
    u = u_val * jax.nn.sigmoid(u_gate)
    y = lax.conv_general_dilated(
        u, w_dw[:, None, :].astype(u.dtype), window_strides=(1,),
        padding=[(CONV_WIDTH - 1, 0)],
        dimension_numbers=("NWC", "WIO", "NWC"),
        feature_group_count=CONV_CH) + b_dw
    yf = y.astype(jnp.float32).reshape(B, S, CONV_GROUPS, CONV_CH // CONV_GROUPS)
    mu = jnp.mean(yf, axis=-1, keepdims=True)
    var = jnp.mean(jnp.square(yf - mu), axis=-1, keepdims=True)
    yf = ((yf - mu) * lax.rsqrt(var + GN_EPS)).reshape(B, S, CONV_CH)
    yf = yf * gn_gain.astype(jnp.float32) + gn_bias.astype(jnp.float32)
    return jax.nn.silu(yf).astype(u_val.dtype)


def _hgrn2(q, fl, i, g, lb, g_out):
    B, S, _ = q.shape
    N = S // CHUNK
    f32 = jnp.float32
    qf = jax.nn.silu(q.astype(f32))
    f = lb + (1.0 - lb) * jax.nn.sigmoid(fl.astype(f32))
    logf = jnp.log(f)
    k = 1.0 - f
    v = i.astype(f32)

    def to_chunks(t, d):
        return t.reshape(B, N, CHUNK, HGRN_HEADS, d).transpose(1, 0, 3, 2, 4)

    qc, kc, lc = to_chunks(qf, HGRN_DK), to_chunks(k, HGRN_DK), to_chunks(logf, HGRN_DK)
    vc = to_chunks(v, HGRN_DV)
    causal = jnp.tril(jnp.ones((CHUNK, CHUNK), dtype=bool))

    def step(state, inp):
        qb, kb, vb, lb_ = inp
        cum = jnp.cumsum(lb_, axis=2)
        diff = cum[:, :, :, None, :] - cum[:, :, None, :, :]
        decay = jnp.exp(jnp.where(causal[None, None, :, :, None], diff, -jnp.inf))
        att = jnp.einsum("bhtk,bhsk,bhtsk->bhts", qb, kb, decay)
        o = jnp.einsum("bhts,bhsv->bhtv", att, vb) + \
            jnp.einsum("bhtk,bhkv->bhtv", qb * jnp.exp(cum), state)
        last = cum[:, :, -1:, :]
        state = jnp.exp(last[:, :, 0, :])[..., None] * state + \
            jnp.einsum("bhsk,bhsv->bhkv", kb * jnp.exp(last - cum), vb)
        return state, o

    s0 = jnp.zeros((B, HGRN_HEADS, HGRN_DK, HGRN_DV), f32)
    _, oc = lax.scan(step, s0, (qc, kc, vc, lc))
    o = oc.transpose(1, 0, 3, 2, 4).reshape(B, S, HGRN_HEADS, HGRN_DV)
    o = _rms(o, g_out.reshape(HGRN_HEADS, HGRN_DV))
    o = o * jax.nn.silu(g.astype(f32).reshape(B, S, HGRN_HEADS, HGRN_DV))
    return o.reshape(B, S, HGRN_WIDTH).astype(q.dtype)


def setup_inputs(seed: int = 0) -> dict:
    key = jax.random.key(seed)
    ks = jax.random.split(key, 20)
    nrm = jax.random.normal
    L, D = DEPTH, D_MODEL
    return {
        "x": nrm(ks[0], (BATCH, SEQ, D), jnp.float32),
        "c": nrm(ks[1], (BATCH, D), jnp.float32),
        "w_ada": nrm(ks[2], (L, D, 6 * D), jnp.float32) * D ** -0.5,
        "b_ada": nrm(ks[3], (L, 6 * D), jnp.float32) * 0.02,
        "lb_logits": nrm(ks[4], (L + 1, HGRN_WIDTH), jnp.float32) * 0.1,
        "g_pre_mix": 1.0 + 0.02 * nrm(ks[5], (L, D), jnp.float32),
        "w_in": nrm(ks[6], (L, D, IN_COLS), jnp.float32) * D ** -0.5,
        "b_in": nrm(ks[7], (L, IN_COLS), jnp.float32) * 0.02,
        "w_dw": nrm(ks[8], (L, CONV_WIDTH, CONV_CH), jnp.float32) * CONV_WIDTH ** -0.5,
        "b_dw": nrm(ks[9], (L, CONV_CH), jnp.float32) * 0.02,
        "gn_gain": 1.0 + 0.02 * nrm(ks[10], (L, CONV_CH), jnp.float32),
        "gn_bias": nrm(ks[11], (L, CONV_CH), jnp.float32) * 0.02,
        "g_hgrn_out": 1.0 + 0.02 * nrm(ks[12], (L, HGRN_WIDTH), jnp.float32),
        "w_out": nrm(ks[13], (L, MIX_WIDTH, D), jnp.float32) * MIX_WIDTH ** -0.5,
        "g_post_mix": 1.0 + 0.02 * nrm(ks[14], (L, D), jnp.float32),
        "g_pre_ffn": 1.0 + 0.02 * nrm(ks[15], (L, D), jnp.float32),
        "w_up": nrm(ks[16], (L, D, D_FF), jnp.float32) * D ** -0.5,
        "w_down": nrm(ks[17], (L, D_FF, D), jnp.float32) * D_FF ** -0.5,
        "g_post_ffn": 1.0 + 0.02 * nrm(ks[18], (L, D), jnp.float32),
    }


def reference(x, c, w_ada, b_ada, lb_logits, g_pre_mix, w_in, b_in, w_dw, b_dw, gn_gain,
              gn_bias, g_hgrn_out, w_out, g_post_mix, g_pre_ffn, w_up, w_down, g_post_ffn):
    lb_all = jnp.cumsum(jax.nn.softmax(lb_logits.astype(jnp.float32), axis=0), axis=0)
    c_act = jax.nn.silu(c)
    for l in range(DEPTH):
        mod = c_act @ w_ada[l] + b_ada[l]
        sh_m, sc_m, gt_m, sh_f, sc_f, gt_f = [m[:, None, :] for m in jnp.split(mod, 6, axis=-1)]

        h = _rms(x, g_pre_mix[l]) * (1.0 + sc_m) + sh_m
        p = h @ w_in[l] + b_in[l]
        o0 = 2 * CONV_CH
        cv = _conv_mixer(p[..., :CONV_CH], p[..., CONV_CH:o0],
                         w_dw[l], b_dw[l], gn_gain[l], gn_bias[l])
        W = HGRN_WIDTH
        hg = _hgrn2(p[..., o0:o0 + W], p[..., o0 + W:o0 + 2 * W],
                    p[..., o0 + 2 * W:o0 + 3 * W], p[..., o0 + 3 * W:o0 + 4 * W],
                    lb_all[l], g_hgrn_out[l])
        y = jnp.concatenate([cv, hg], axis=-1) @ w_out[l]
        x = x + gt_m * _rms(y, g_post_mix[l])

        h = _rms(x, g_pre_ffn[l]) * (1.0 + sc_f) + sh_f
        y = jnp.square(jax.nn.relu(h @ w_up[l])) @ w_down[l]
        x = x + gt_f * _rms(y, g_post_ffn[l])
    return x
```

```python
import numpy as np
import concourse.bass as bass
import concourse.mybir as mybir
from concourse.bass_utils import run_bass_kernel_spmd

F32 = mybir.dt.float32
BF16 = mybir.dt.bfloat16
AF = mybir.ActivationFunctionType
ALU = mybir.AluOpType

D = 1024
NCORES = 8
SEQ = 4096
T = 256
NSUB = T // 128
RMS_EPS = 1e-6
PH1_MODEL = True
ATTACH_WAITS = True
SIGM_DVE = False
SIGM_POOL = True
GN_EPS = 1e-5
ENGS = ("pe", "act", "dve", "pool", "sp")


class Buf:
    __slots__ = ("name", "lastw", "readers")

    def __init__(self, name):
        self.name = name
        self.lastw = None
        self.readers = []


class V:
    __slots__ = ("ap", "bufs")

    def __init__(self, ap, bufs):
        self.ap = ap
        self.bufs = tuple(bufs)

    def __getitem__(self, idx):
        return V(self.ap[idx], self.bufs)


class Op:
    __slots__ = ("eng", "fn", "raw", "war", "idx", "inc", "semval", "dma_key", "dma_val", "waits", "t_end")

    def __init__(self, eng, fn):
        self.eng = eng
        self.fn = fn
        self.raw = []
        self.war = []
        self.idx = -1
        self.inc = False
        self.semval = 0
        self.dma_key = None
        self.dma_val = 0
        self.waits = []
        self.t_end = 0.0


class Prog:
    def __init__(self, nc):
        self.nc = nc
        self.eng_ops = {e: [] for e in ENGS}
        self.all_ops = []
        self.dma_count = {}
        self.dma_last = {}
        self.final_waits = []
        self.eng_free = {e: 0.0 for e in ENGS}
        self.log = None

    def _add(self, eng, fn, reads, writes, dma_key=None, cost=100.0):
        op = Op(eng, fn)
        rb, wb = [], []
        for v in reads:
            if v is None or isinstance(v, (int, float)):
                continue
            for b in (v.bufs if isinstance(v, V) else (v,)):
                if b not in rb:
                    rb.append(b)
        for v in writes:
            for b in (v.bufs if isinstance(v, V) else (v,)):
                if b not in wb:
                    wb.append(b)
        for b in rb:
            if b.lastw is not None and b.lastw not in op.raw:
                op.raw.append(b.lastw)
        for b in wb:
            if b.lastw is not None and b.lastw not in op.raw:
                op.raw.append(b.lastw)
            for r in b.readers:
                if r is not op and r not in op.war and r not in op.raw:
                    op.war.append(r)
        log = self.log
        for b in rb:
            if log is not None:
                log.append((0, b, len(b.readers)))
            b.readers.append(op)
        for b in wb:
            if log is not None:
                log.append((1, b, b.lastw, b.readers))
            b.lastw = op
            b.readers = []
        op.idx = len(self.eng_ops[eng])
        self.eng_ops[eng].append(op)
        self.all_ops.append(op)
        if dma_key is not None:
            op.dma_key = dma_key
            if log is not None:
                log.append((2, dma_key, self.dma_count.get(dma_key), self.dma_last.get(dma_key)))
            self.dma_count[dma_key] = self.dma_count.get(dma_key, 0) + 16
            op.dma_val = self.dma_count[dma_key]
            prev = self.dma_last.get(dma_key)
            if prev is not None and prev not in op.raw:
                op.raw.append(prev)
            self.dma_last[dma_key] = op
        t0 = self.eng_free[eng]
        for p in op.raw:
            lat = 60.0 if (p.eng == eng and p.dma_key is None) else 180.0
            if not (p.eng == eng == "pe"):
                t0 = max(t0, p.t_end + lat)
        for p in op.war:
            if p.eng != eng or p.dma_key is not None:
                t0 = max(t0, p.t_end + 180.0)
        if dma_key is not None:
            self.eng_free[eng] = t0 + 60.0
            op.t_end = t0 + cost
        else:
            op.t_end = t0 + cost
            self.eng_free[eng] = op.t_end
        return op

    def checkpoint(self):
        assert self.log is None
        self.log = []
        return ({e: len(self.eng_ops[e]) for e in ENGS}, dict(self.eng_free), len(self.final_waits), len(self.all_ops))

    def rollback(self, cp_):
        lens, free, nfw, nall = cp_
        for ent in reversed(self.log):
            if ent[0] == 0:
                del ent[1].readers[ent[2]:]
            elif ent[0] == 1:
                ent[1].lastw = ent[2]
                ent[1].readers = ent[3]
            else:
                k = ent[1]
                if ent[2] is None:
                    self.dma_count.pop(k, None)
                    self.dma_last.pop(k, None)
                else:
                    self.dma_count[k] = ent[2]
                    self.dma_last[k] = ent[3]
        for e in ENGS:
            del self.eng_ops[e][lens[e]:]
        del self.all_ops[nall:]
        self.eng_free = free
        del self.final_waits[nfw:]
        self.log = None

    def commit(self):
        self.log = None

    def interleave(self, streams, ready=None):
        pos = [0] * len(streams)
        while True:
            alive = [i for i in range(len(streams)) if pos[i] < len(streams[i])]
            if not alive:
                break
            cand = [i for i in alive if ready is None or ready(i, pos)]
            assert cand, "interleave: no stream is ready"
            if len(cand) == 1:
                i = cand[0]
                streams[i][pos[i]]()
                pos[i] += 1
                continue
            best, best_m = None, None
            for i in cand:
                pe0 = self.eng_free["pe"]
                n0 = len(self.eng_ops["pe"])
                cp_ = self.checkpoint()
                streams[i][pos[i]]()
                work = 0.0
                tprev = pe0
                idle = 0.0
                for op in self.eng_ops["pe"][n0:]:
                    idle += 0.0
                pe1 = self.eng_free["pe"]
                work = sum(op.semval for op in self.eng_ops["pe"][n0:])
                metric = (pe1 - pe0) - work
                if n0 == len(self.eng_ops["pe"]):
                    metric = -1.0
                self.rollback(cp_)
                if best is None or metric < best_m - 1.0:
                    best, best_m = i, metric
            streams[best][pos[best]]()
            pos[best] += 1

    def op(self, eng, fn, reads=(), writes=(), cost=100.0):
        o = self._add(eng, fn, reads, writes, cost=cost)
        o.semval = cost
        return o

    def dma(self, eng, out, in_, key, is_output=False):
        o_ap = out.ap if isinstance(out, V) else out
        i_ap = in_.ap if isinstance(in_, V) else in_
        reads = [in_] if isinstance(in_, V) else []
        writes = [out] if isinstance(out, V) else []
        nbytes = 1
        for d_ in o_ap.shape:
            nbytes *= int(d_)
        nbytes *= 4
        op = self._add(eng, lambda e: e.dma_start(out=o_ap, in_=i_ap), reads, writes, dma_key=key,
                       cost=2000.0 + nbytes / 150.0)
        if is_output:
            self.final_waits.append(op)
        return op

    def emit(self):
        nc = self.nc
        know = {e: {} for e in ENGS}
        opknow = {}
        for op in self.all_ops:
            e = op.eng
            ke = know[e]
            need = {}
            for is_war, plist in ((False, op.raw), (True, op.war)):
                for p in plist:
                    if p.dma_key is not None:
                        k_, v_ = "d:" + str(p.dma_key), p.dma_val
                    else:
                        if p.eng == e and e == "pe":
                            continue
                        k_, v_ = p.eng, p.idx
                    if ke.get(k_, -1) < v_ and need.get(k_, (-1, None))[0] < v_:
                        need[k_] = (v_, p)
            op.waits = []
            for k_, (v_, p) in need.items():
                if ke.get(k_, -1) >= v_:
                    continue
                if p.dma_key is not None:
                    op.waits.append(("dma", p.dma_key, v_))
                else:
                    p.inc = True
                    op.waits.append(("eng", p.eng, p))
                ke[k_] = max(ke.get(k_, -1), v_)
                for kk_, vv_ in opknow[id(p)].items():
                    if ke.get(kk_, -1) < vv_:
                        ke[kk_] = vv_
            kn = dict(ke)
            if op.dma_key is None:
                kn[e] = max(kn.get(e, -1), op.idx)
            opknow[id(op)] = kn
        seen_dma = {"sp": {k_[2:]: v_ for k_, v_ in know["sp"].items() if isinstance(k_, str) and k_.startswith("d:")}}
        fw = {}
        for p in self.final_waits:
            if seen_dma["sp"].get(p.dma_key, 0) < p.dma_val:
                fw[p.dma_key] = max(fw.get(p.dma_key, 0), p.dma_val)
        for e in ENGS:
            c = 0
            for op in self.eng_ops[e]:
                if op.inc:
                    c += 1
                    op.semval = c
        esem = {e: nc.alloc_semaphore(name="sem_" + e) for e in ENGS}
        dsem = {k: nc.alloc_semaphore(name="dsem_" + str(k)) for k in self.dma_count}
        handles = {"pe": "tensor", "act": "scalar", "dve": "vector", "pool": "gpsimd", "sp": "sync"}
        self.n_waits = 0
        self.n_sems = len(esem) + len(dsem)

        def run_engine(e, eng):
            for op in self.eng_ops[e]:
                ws = op.waits
                sep = ws[:-1] if ATTACH_WAITS else ws
                for w in sep:
                    self.n_waits += 1
                    if w[0] == "eng":
                        eng.wait_ge(esem[w[1]], w[2].semval)
                    else:
                        eng.wait_ge(dsem[w[1]], w[2])
                ins = op.fn(eng)
                if ATTACH_WAITS and ws:
                    w = ws[-1]
                    if w[0] == "eng":
                        ins = ins._wait_ge(esem[w[1]], w[2].semval)
                    else:
                        ins = ins._wait_ge(dsem[w[1]], w[2])
                if op.dma_key is not None:
                    ins.then_inc(dsem[op.dma_key], 16)
                elif op.inc:
                    ins.then_inc(esem[e], 1)
            if e == "sp":
                for k, v in fw.items():
                    eng.wait_ge(dsem[k], v)

        with nc.Block() as block:
            for e in ENGS:
                deco = getattr(block, handles[e])

                def body(eng, e=e):
                    run_engine(e, eng)

                deco(body)


NC_COL = 8 + 12 + 124 + 12
NR_ROW = 4096 + 1024 + 512
NK_CONST = 128 * 5 + 512


class _Stop(Exception):
    pass


def build_nc(S, stop=0):
    holder = {}
    try:
        return _build(S, stop, holder)
    except _Stop:
        return holder["nc"], holder["P"]


def _build(S, stop, holder):
    assert S % T == 0
    NM = S // T
    nc = bass.Bass("TRN2", target_bir_lowering=False)
    P = Prog(nc)
    holder["nc"] = nc
    holder["P"] = P

    x_d = nc.dram_tensor("x", [S, D], F32, kind="ExternalInput").ap()
    wada_d = nc.dram_tensor("w_ada", [D, 6 * D], F32, kind="ExternalInput").ap()
    win_d = nc.dram_tensor("w_in", [D, 3072], F32, kind="ExternalInput").ap()
    wout_d = nc.dram_tensor("w_out", [D, D], F32, kind="ExternalInput").ap()
    wup_d = nc.dram_tensor("w_up", [D, 4 * D], F32, kind="ExternalInput").ap()
    wdn_d = nc.dram_tensor("w_down", [4 * D, D], F32, kind="ExternalInput").ap()
    col_d = nc.dram_tensor("colpack", [128, NC_COL], F32, kind="ExternalInput").ap()
    row_d = nc.dram_tensor("rowpack", [1, NR_ROW], F32, kind="ExternalInput").ap()
    bada_d = nc.dram_tensor("b_ada_row", [1, 6 * D], F32, kind="ExternalInput").ap()
    bint_d = nc.dram_tensor("b_in_tok", [1, 1536], F32, kind="ExternalInput").ap()
    cst_d = nc.dram_tensor("consts", [128, NK_CONST], F32, kind="ExternalInput").ap()
    out_d = nc.dram_tensor("out", [S, D], F32, kind="ExternalOutput").ap()
    x1_d = nc.dram_tensor("x1_scratch", [S, D], F32).ap()
    modf_d = nc.dram_tensor("modf_scratch", [128, 3 * D], F32).ap()
    x1_bufs = [Buf("x1d%d" % i) for i in range(S // 128)]
    modf_buf = Buf("modfd")

    ARENA = 53150
    arena = nc.alloc_sbuf_tensor("arena", [128, ARENA], F32).ap()

    class Region:
        def __init__(self, start, end):
            self.start, self.end, self.ptr = start, end, start
            self.bufs = []

    def alloc(reg, nbytes, dt=F32, name="t", parts=128, shape=None, nbuf=1):
        words = (nbytes + 3) // 4
        off = reg.ptr
        reg.ptr += words
        assert reg.ptr <= reg.end, ("arena region overflow", name, reg.ptr, reg.end)
        ap = arena[0:parts, off:off + words]
        if dt != F32:
            ap = ap.bitcast(dt)
        if shape is not None:
            ap = ap.rearrange(shape[0], **shape[1])
        bufs = [Buf(name + str(i)) for i in range(nbuf)]
        reg.bufs.extend(bufs)
        return V(ap, bufs)

    def sub(v, idx, bi):
        return V(v.ap[idx], [v.bufs[bi]])

    COMMON_W = 10200
    R_common = Region(0, COMMON_W)
    R_ph1w = Region(COMMON_W, COMMON_W + 26700)
    R_ph1t = Region(COMMON_W + 26700, ARENA)
    R_st = Region(COMMON_W + 26700, ARENA)
    R_ph2 = Region(COMMON_W, ARENA)

    CST = alloc(R_common, NK_CONST * 4, name="cst")
    IDENT = CST[:, 0:128]
    TRI = CST[:, 128:256]
    CC = CST[:, 256:384]
    GM = CST[:, 384:512]
    ONES = CST[:, 512:640]
    MASK4 = CST[:, 640:1152]
    IDB = alloc(R_common, 256, BF16, "idb")
    ONESB = alloc(R_common, 256, BF16, "onesb")
    COL = alloc(R_common, NC_COL * 4, name="col")
    NEGB = alloc(R_common, 12 * 4, name="negb")
    G1 = alloc(R_common, 4096, name="G1")
    SH1 = alloc(R_common, 4096, name="SH1")
    GT1 = alloc(R_common, 4096, name="GT1")
    XS = [alloc(R_common, 4096, name="X%d" % i) for i in range(4)]
    HF = alloc(R_common, 4096, name="HF")
    HB = alloc(R_common, 2048, BF16, name="HB")
    STAT = alloc(R_common, 64, name="stat")
    c_col = COL[:, 0:8]
    bfm = COL[:, 8:20]
    wdwT = COL[:, 20:144]
    bdw = COL[:, 144:148]
    gng = COL[:, 148:152]
    gnb = COL[:, 152:156]

    WIN = alloc(R_ph1w, 8 * 3072 * 2, BF16, "win", shape=("p (k n) -> p k n", dict(k=8)), nbuf=6)
    WOUT = alloc(R_ph1w, 8 * 1024 * 2, BF16, "wout", shape=("p (k n) -> p k n", dict(k=8)), nbuf=2)
    DG = alloc(R_ph1w, 124 * 128 * 2, BF16, "dg", shape=("p (c m) -> p c m", dict(c=124)), nbuf=4)
    OMLB = alloc(R_ph1w, 2048, name="omlb")
    GOUT = alloc(R_ph1w, 2048, name="gout")
    BINT = alloc(R_ph1w, 1536 * 2, BF16, "bint", parts=1)
    STATE = alloc(R_ph1w, 2048, name="state", shape=("p (h v) -> p h v", dict(h=4)))

    HT2 = [alloc(R_ph1t, 8 * T * 2, BF16, "hT%d" % i, shape=("p (k t) -> p k t", dict(k=8)), nbuf=NSUB) for i in range(2)]
    HFP = alloc(R_ph1t, 4096, name="hfp")
    HBP = alloc(R_ph1t, 2048, BF16, name="hbp")
    STAT2 = alloc(R_ph1t, 64, name="stat2")
    UE = [alloc(R_ph1t, 4 * (30 + T) * 2, BF16, "ue%d" % i, shape=("p (c t) -> p c t", dict(c=4)), nbuf=4)
          for i in range(2)]
    TA = [alloc(R_ph1t, T * 4, name="ta%d" % i) for i in range(2)]
    QS2 = [alloc(R_ph1t, 4 * T * 4, name="qs%d" % i, shape=("p (h t) -> p h t", dict(h=4)), nbuf=4) for i in range(2)]
    YSB = alloc(R_ph1t, T * 4, name="ysb")
    Y2SB = alloc(R_ph1t, T * 4, name="y2sb")
    M2 = alloc(R_ph1t, T * 4, name="m2")
    VAR = alloc(R_ph1t, T * 4, name="var")
    DD = alloc(R_ph1t, T * 4, name="dd")
    MIXT2 = [alloc(R_ph1t, 8 * T * 2, BF16, "mixT%d" % i, shape=("p (k t) -> p k t", dict(k=8)), nbuf=4 + NSUB) for i in range(2)]
    A1 = alloc(R_ph1t, 2048, name="a1")
    KK = alloc(R_ph1t, 2048, name="kk")
    LF = alloc(R_ph1t, 2048, name="lf")
    KT = alloc(R_ph1t, 1024, BF16, "kt")
    KTT = alloc(R_ph1t, 1024, BF16, "ktt", shape=("p (h t) -> p h t", dict(h=4)))
    VB = alloc(R_ph1t, 1024, BF16, "vb")
    SMALL = alloc(R_ph1t, 6 * 16, name="small")
    EP = alloc(R_ph1t, 2048, name="ep", shape=("p (h t) -> p h t", dict(h=4)))
    QT = alloc(R_ph1t, 1024, BF16, "qt", shape=("p (h t) -> p h t", dict(h=4)))
    ATM = alloc(R_ph1t, 1024, BF16, "atm", shape=("p (h t) -> p h t", dict(h=4)))
    SM = alloc(R_ph1t, 1024, BF16, "sm", shape=("p (h v) -> p h v", dict(h=4)))
    TMPS = V(KK.ap.rearrange("p (h v) -> p h v", h=4), KK.bufs)
    T1 = LF
    SG = A1
    HG = alloc(R_ph1t, 1024, BF16, "hg")
    GF = alloc(R_ph1t, 2048, name="gf")
    ONES512 = alloc(R_ph1t, 2048, name="ones512")
    NEGONES512 = alloc(R_ph1t, 2048, name="negones512")

    BADA = alloc(R_st, 6 * D * 2, BF16, "bada", parts=1)
    ROW = alloc(R_st, NR_ROW * 4, name="row", parts=1)
    CB = alloc(R_st, 8 * 128 * 2, BF16, "cb", shape=("p (k m) -> p k m", dict(k=8)))
    CACT = alloc(R_st, 64, name="cact")
    LBT = alloc(R_st, 2048, name="lbt", parts=1)
    WA = [alloc(R_st, 8 * 512 * 2, BF16, "wa%d" % i, shape=("p (k n) -> p k n", dict(k=8))) for i in range(2)]

    WUP = alloc(R_ph2, 8 * 4096 * 2, BF16, "wup", shape=("p (k n) -> p k n", dict(k=8)), nbuf=8)
    WDN = alloc(R_ph2, 32 * 1024 * 2, BF16, "wdn", shape=("p (k n) -> p k n", dict(k=32)), nbuf=8)
    G2 = alloc(R_ph2, 4096, name="G2")
    SH2 = alloc(R_ph2, 4096, name="SH2")
    GT2 = alloc(R_ph2, 4096, name="GT2")
    H2T2 = [alloc(R_ph2, 8 * T * 2, BF16, "h2T%d" % i, shape=("p (k t) -> p k t", dict(k=8)), nbuf=NSUB) for i in range(2)]
    STAT3 = alloc(R_ph2, 64, name="stat3")
    AT = alloc(R_ph2, 32 * T * 2, BF16, "aT", shape=("p (f t) -> p f t", dict(f=32)), nbuf=32)
    RR = [alloc(R_ph2, T * 4, name="rr%d" % i) for i in range(3)]

    psum = nc.alloc_psum_tensor("psum", [128, 4096], F32).ap()

    bankbuf = [Buf("bank%d" % i) for i in range(8)]

    def pv(b, lo=0, hi=512):
        return V(psum[:, b * 512 + lo:b * 512 + hi], [bankbuf[b]])

    PS_TR = pv(0)
    PS_TRB = V(psum[:, 0:512].bitcast(BF16), PS_TR.bufs)
    PS_GNM = pv(0, 0, 256)
    PS_GNE = pv(0, 256, 512)
    PS_FMA = pv(1, 0, 256)
    PS_FMB = pv(2, 0, 256)
    PS_CVA = pv(1, 0, 256)
    PS_CVB = pv(2, 0, 256)
    PS_TMA = pv(3)
    PS_TMB = pv(4)
    PS_TMBB = V(psum[:, 4 * 512:5 * 512].bitcast(BF16), PS_TMB.bufs)
    PS_Y = V(psum[:, 3 * 512:5 * 512], [bankbuf[3], bankbuf[4]])
    PS_CT = pv(5)
    PS_CTB = V(psum[:, 5 * 512:6 * 512].bitcast(BF16), PS_CT.bufs)
    PS_CF = pv(6)
    PS_O = pv(7)

    def apof(v):
        return v.ap if isinstance(v, V) else v

    def fsz(v):
        n = 1
        for d_ in v.ap.shape[1:]:
            n *= int(d_)
        return n

    def is_ps(v):
        return isinstance(v, V) and v.bufs and v.bufs[0].name.startswith("bank")

    def mm(out, lhsT, rhs, start, stop):
        n = max(64, fsz(rhs)) * (4 if rhs.ap.dtype == F32 else 1)
        P.op("pe", lambda e: e.matmul(out.ap, lhsT=lhsT.ap, rhs=rhs.ap, start=start, stop=stop), [lhsT, rhs], [out],
             cost=n / 1.95 + 15.0)

    def tr(out, in_, ident):
        P.op("pe", lambda e: e.transpose(out.ap, in_.ap, ident.ap), [in_, ident], [out], cost=110.0)

    def act(out, in_, func, bias=None, scale=None, accum=None):
        kw = {}
        if bias is not None:
            kw["bias"] = apof(bias)
        if scale is not None:
            kw["scale"] = apof(scale)
        if accum is not None:
            kw["accum_out"] = accum.ap
        w = [out] + ([accum] if accum is not None else [])
        P.op("act", lambda e: e.activation(out=out.ap, in_=in_.ap, func=func, **kw), [in_, bias, scale], w,
             cost=(224.0 + fsz(out)) / 1.2)

    def sigm(out, in_, negbias=None, pos=True, pool=False):
        if pos:
            act(out, in_, AF.Exp, bias=negbias, scale=-1.0)
        else:
            assert negbias is None
            act(out, in_, AF.Exp, scale=1.0)
        if pool and SIGM_POOL:
            n_ = fsz(out)
            tt("pool", out, out, ONES512[:, 0:n_], ALU.add)
            tt("pool", out, out, NEGONES512[:, 0:n_], ALU.pow)
        elif SIGM_DVE:
            ts("dve", out, out, 1.0, None, ALU.add)
            P.op("dve", lambda e: e.reciprocal(out=out.ap, in_=out.ap), [out], [out], cost=(60.0 + fsz(out)) / 0.96)
        else:
            act(out, out, AF.Ln, bias=1.0, scale=1.0)
            act(out, out, AF.Exp, scale=-1.0)

    def rsqrt_small(out, in_, eps):
        act(out, in_, AF.Ln, bias=eps, scale=1.0)
        act(out, out, AF.Exp, scale=-0.5)

    def vcost(eng, out, ins, fast=False):
        f = fsz(out)
        if eng == "pool":
            return 100.0 + 2.2 * f
        base = 120.0 if any(is_ps(v) for v in ins) else 60.0
        if fast and base == 60.0:
            f = f / 2.0
        return (base + f) / 0.96

    def tt(eng, out, in0, in1, op):
        P.op(eng, lambda e: e.tensor_tensor(out=out.ap, in0=in0.ap, in1=in1.ap, op=op), [in0, in1], [out],
             cost=vcost(eng, out, [in0, in1]))

    def stt(eng, out, in0, scalar, in1, op0, op1):
        P.op(eng, lambda e: e.scalar_tensor_tensor(out=out.ap, in0=in0.ap, scalar=apof(scalar), in1=in1.ap,
                                                    op0=op0, op1=op1), [in0, scalar, in1], [out],
             cost=vcost(eng, out, [in0, in1]))

    def ts(eng, out, in0, s1, s2, op0, op1=None):
        if op1 is None:
            P.op(eng, lambda e: e.tensor_scalar(out=out.ap, in0=in0.ap, scalar1=apof(s1), scalar2=None, op0=op0),
                 [in0, s1], [out], cost=vcost(eng, out, [in0], True))
        else:
            P.op(eng, lambda e: e.tensor_scalar(out=out.ap, in0=in0.ap, scalar1=apof(s1), scalar2=apof(s2),
                                                 op0=op0, op1=op1), [in0, s1, s2], [out], cost=vcost(eng, out, [in0], True))

    def cp(eng, out, in_):
        if eng == "act":
            P.op("act", lambda e: e.copy(out=out.ap, in_=in_.ap), [in_], [out], cost=(224.0 + fsz(out)) / 1.2)
        else:
            P.op(eng, lambda e: e.tensor_copy(out=out.ap, in_=in_.ap), [in_], [out], cost=vcost(eng, out, [in_], True))

    def memset(eng, out, val):
        P.op(eng, lambda e: e.memset(out.ap, val), [], [out])

    def fence(eng, old_bufs, new_bufs):
        P.op(eng, lambda e: e.memset(STAT.ap[:, 15:16], 0.0), [], list(old_bufs) + list(new_bufs))

    def chk(k, dumps):
        if stop != k:
            return
        r = 0
        for v in dumps:
            n = v.ap.shape[-1]
            P.dma("sp", out_d[r:r + 128, 0:n], v, key="st0", is_output=True)
            r += 128
        P.emit()
        raise _Stop()

    P.dma("sp", CST, cst_d, key="c0")
    P.dma("sp", COL, col_d, key="c1")
    P.dma("sp", ROW, row_d, key="c2")
    P.dma("pool", BADA, bada_d, key="c3")
    P.dma("pool", BINT, bint_d, key="c4")
    xkeys = ["x0", "x1", "x2", "x3"]
    for s_ in range(NSUB):
        P.dma("sp", XS[s_], x_d[s_ * 128:(s_ + 1) * 128, :], key=xkeys[s_])

    wada_v = wada_d.rearrange("(k p) n -> p k n", p=128)
    win_v = win_d.rearrange("(k p) n -> p k n", p=128)
    wout_v = wout_d.rearrange("(k p) n -> p k n", p=128)
    wup_v = wup_d.rearrange("(k p) n -> p k n", p=128)
    wdn_v = wdn_d.rearrange("(k p) n -> p k n", p=128)

    cp("dve", IDB, IDENT)
    memset("dve", ONESB, 1.0)
    ts("dve", NEGB, bfm, -1.0, None, ALU.mult)
    memset("dve", STATE, 0.0)
    sigm(CACT[:, 0:8], c_col)
    tt("dve", CACT[:, 8:16], CACT[:, 0:8], c_col, ALU.mult)
    for k in range(8):
        ts("dve", CB[:, k, :], ONES, CACT[:, 8 + k:9 + k], None, ALU.mult)
    ones_row = ONES[0:1, :]
    onesb_row = ONESB[0:1, :]

    def bcast_row(dst, row_slice, ps):
        mm(ps, ones_row, row_slice, True, True)
        cp("act", dst, ps)

    bcast_row(G1[:, 0:512], ROW[:, 0:512], PS_TMA)
    bcast_row(G1[:, 512:1024], ROW[:, 512:1024], PS_TMB)
    bcast_row(GT1[:, 0:512], ROW[:, 1024:1536], PS_TMA)
    bcast_row(GT1[:, 512:1024], ROW[:, 1536:2048], PS_TMB)
    bcast_row(GOUT, ROW[:, 5120:5632], PS_TMA)
    tt("dve", LBT, ROW[:, 4096:4608], ROW[:, 4608:5120], ALU.subtract)
    act(LBT, LBT, AF.Exp, scale=1.0)
    act(LBT, LBT, AF.Ln, bias=1.0, scale=1.0)
    act(LBT, LBT, AF.Exp, scale=-1.0)
    bcast_row(OMLB, LBT, PS_TMB)
    for c in range(4):
        for j in range(31):
            ts("dve", sub(DG, (slice(None), c * 31 + j, slice(None)), c), IDENT,
               wdwT[:, c * 31 + j:c * 31 + j + 1], None, ALU.mult)

    PS_CVA_full = PS_CT
    mod_ps = [PS_TMA, PS_TMB]
    wcount = [0]

    def mod_chunk(n, consume):
        wa = WA[wcount[0] % 2]
        ps = mod_ps[wcount[0] % 2]
        wcount[0] += 1
        P.dma("pool", wa, wada_v[:, :, n * 512:(n + 1) * 512], key="wa%d" % (wcount[0] % 2))
        mm(ps, onesb_row, BADA[:, n * 512:(n + 1) * 512], True, False)
        for k in range(8):
            mm(ps, CB[:, k, :], wa[:, k, :], False, k == 7)
        consume(ps)

    def half(vv, n):
        return vv[:, (n % 2) * 512:(n % 2) * 512 + 512]

    for n in (2, 3):
        mod_chunk(n, lambda ps, n=n: stt("dve", half(G1, n), ps, 1.0, half(G1, n), ALU.add, ALU.mult))
    for n in (0, 1):
        mod_chunk(n, lambda ps, n=n: cp("act", half(SH1, n), ps))
    for j in range(6):
        P.dma("pool", sub(WIN, (slice(None), slice(None), slice(j * 512, (j + 1) * 512)), j),
              win_v[:, :, j * 512:(j + 1) * 512], key="win%d" % j)
    for n in (4, 5):
        mod_chunk(n, lambda ps, n=n: tt("dve", half(GT1, n), ps, half(GT1, n), ALU.mult))
    for j in range(2):
        P.dma("pool", sub(WOUT, (slice(None), slice(None), slice(j * 512, (j + 1) * 512)), j),
              wout_v[:, :, j * 512:(j + 1) * 512], key="wout%d" % j)
    for n in (8, 9):
        bcast_row(HF[:, 0:512], ROW[:, 2048 + (n % 2) * 512:2048 + (n % 2) * 512 + 512], PS_CVA_full)
        mod_chunk(n, lambda ps, n=n: stt("dve", HF[:, 0:512], ps, 1.0, HF[:, 0:512], ALU.add, ALU.mult))
        P.dma("sp", V(modf_d[:, (n % 2) * 512:(n % 2) * 512 + 512], [modf_buf]), HF[:, 0:512], key="mf")
    for n in (6, 7):
        mod_chunk(n, lambda ps, n=n: cp("act", HF[:, 0:512], ps))
        P.dma("sp", V(modf_d[:, 1024 + (n % 2) * 512:1024 + (n % 2) * 512 + 512], [modf_buf]), HF[:, 0:512], key="mf")
    for n in (10, 11):
        bcast_row(HF[:, 0:512], ROW[:, 3072 + (n % 2) * 512:3072 + (n % 2) * 512 + 512], PS_CVA_full)
        mod_chunk(n, lambda ps, n=n: tt("dve", HF[:, 0:512], ps, HF[:, 0:512], ALU.mult))
        P.dma("sp", V(modf_d[:, 2048 + (n % 2) * 512:2048 + (n % 2) * 512 + 512], [modf_buf]), HF[:, 0:512], key="mf")

    chk(1, [G1, SH1, GT1, OMLB])
    fence("dve", R_st.bufs, R_ph1t.bufs)
    memset("dve", UE[0], 0.0)
    memset("dve", ATM, 0.0)
    memset("dve", ONES512, 1.0)
    memset("dve", NEGONES512, -1.0)

    def norm_a(X, Gt, SHt):
        ms = STAT[:, 0:1]
        rs = STAT[:, 1:2]
        act(HB, X, AF.Square, scale=1.0 / 32.0, accum=ms)
        rsqrt_small(rs, ms, RMS_EPS)
        stt("dve", HF, X, rs, Gt, ALU.mult, ALU.mult)
        tt("dve", HB, HF, SHt, ALU.add)

    def norm_b(hT, s_):
        for k in range(8):
            tr(PS_TRB[:, k * 128:(k + 1) * 128], HB[:, k * 128:(k + 1) * 128], IDB)
        dst = V(hT.ap[:, :, s_ * 128:(s_ + 1) * 128], [hT.bufs[s_]])
        src = V(PS_TRB.ap.rearrange("p (k t) -> p k t", k=8), PS_TRB.bufs)
        cp("act", dst, src)

    def norm_stage(X, Gt, SHt, hT, s_):
        norm_a(X, Gt, SHt)
        norm_b(hT, s_)

    def post_stage(Y, X, GTt, hf=None, hb=None, st=None):
        hf = HF if hf is None else hf
        hb = HB if hb is None else hb
        st = STAT if st is None else st
        ms = st[:, 2:3]
        rs = st[:, 3:4]
        if hb is PS_TR:
            act(hb, Y[:, 0:512], AF.Square, scale=1.0 / 32.0, accum=st[:, 4:5])
            act(hb, Y[:, 512:1024], AF.Square, scale=1.0 / 32.0, accum=st[:, 5:6])
            tt("dve", ms, st[:, 4:5], st[:, 5:6], ALU.add)
        else:
            act(hb, Y, AF.Square, scale=1.0 / 32.0, accum=ms)
        rsqrt_small(rs, ms, RMS_EPS)
        if hf is Y:
            for h_ in range(2):
                stt("dve", Y[:, h_ * 512:(h_ + 1) * 512], Y[:, h_ * 512:(h_ + 1) * 512], rs, GTt[:, h_ * 512:(h_ + 1) * 512],
                    ALU.mult, ALU.mult)
        else:
            stt("dve", hf, Y, rs, GTt, ALU.mult, ALU.mult)
        tt("dve", X, X, hf, ALU.add)

    SMv = SMALL
    NEGMID, EM, EL, SS4, RS4 = (SMv[:, 0:4], SMv[:, 4:8], SMv[:, 8:12], SMv[:, 12:16], SMv[:, 16:20])

    def xs_of(m):
        return [XS[(m % 2) * NSUB + s_] for s_ in range(NSUB)]

    def steps_A(m):
        st = []
        HT = HT2[m % 2]

        def pre():
            if m >= 1:
                for s_ in range(NSUB):
                    slot = (m % 2) * NSUB + s_
                    r0 = m * T + s_ * 128
                    P.dma("sp", XS[slot], x_d[r0:r0 + 128, :], key=xkeys[slot])
        st.append(pre)
        for s_ in range(NSUB):
            st.append(lambda s_=s_: norm_a(xs_of(m)[s_], G1, SH1))
            st.append(lambda s_=s_: norm_b(HT, s_))
        return st

    def steps_B(m):
        st = []
        HT = HT2[m % 2]
        QS = QS2[m % 2]
        ue = UE[m % 2]
        ue_next = UE[(m + 1) % 2]
        for c in range(4):
            def g1(c=c):
                wg = sub(WIN, (slice(None), slice(None), slice(512 + c * 128, 512 + (c + 1) * 128)), 1)
                for k in range(8):
                    mm(PS_FMA, wg[:, k, :], HT[:, k, :], k == 0, k == 7)
                sigm(TA[c % 2], PS_FMA, negbias=NEGB[:, 4 + c:5 + c], pool=True)

            def g2(c=c):
                wv = sub(WIN, (slice(None), slice(None), slice(c * 128, (c + 1) * 128)), 0)
                for k in range(8):
                    mm(PS_FMB, wv[:, k, :], HT[:, k, :], k == 0, k == 7)
                stt("dve", sub(ue, (slice(None), c, slice(30, 30 + T)), c), PS_FMB, bfm[:, c:c + 1], TA[c % 2], ALU.add, ALU.mult)
                cp("pool", sub(ue_next, (slice(None), c, slice(0, 30)), c), sub(ue, (slice(None), c, slice(T, T + 30)), c))
            st.append(g1)
            st.append(g2)
        for hd in range(4):
            def gq(hd=hd):
                wq = sub(WIN, (slice(None), slice(None), slice(1024 + hd * 128, 1024 + (hd + 1) * 128)), 2)
                ps = PS_FMA if hd % 2 == 0 else PS_FMB
                for k in range(8):
                    mm(ps, wq[:, k, :], HT[:, k, :], k == 0, k == 7)
                ta = TA[hd % 2]
                sigm(ta, ps, negbias=NEGB[:, 8 + hd:9 + hd], pool=True)
                stt("dve", sub(QS, (slice(None), hd, slice(None)), hd), ps, bfm[:, 8 + hd:9 + hd], ta, ALU.add, ALU.mult)
            st.append(gq)
        return st

    def steps_C(m):
        st = []
        ue = UE[m % 2]
        MIXT = MIXT2[m % 2]
        for c in range(4):
            pcv = PS_CVA if c % 2 == 0 else PS_CVB

            def cv(c=c, pcv=pcv, lo=0, hi=31):
                for j in range(lo, hi):
                    mm(pcv, sub(DG, (slice(None), c * 31 + j, slice(None)), c), sub(ue, (slice(None), c, slice(j, j + T)), c),
                       j == 0, j == 30)
            st.append(lambda c=c, pcv=pcv: cv(c, pcv, 0, 16))
            st.append(lambda c=c, pcv=pcv: cv(c, pcv, 16, 31))

            def gn0(c=c, pcv=pcv):
                act(YSB, pcv, AF.Identity, bias=bdw[:, c:c + 1], scale=1.0)
                act(Y2SB, pcv, AF.Square, bias=bdw[:, c:c + 1], scale=1.0)
            st.append(gn0)

            def gn(c=c, pcv=pcv):
                mm(PS_GNM, GM, YSB, True, True)
                mm(PS_GNE, GM, Y2SB, True, True)
                act(M2, PS_GNM, AF.Square)
                tt("dve", VAR, PS_GNE, M2, ALU.subtract)
                rsqrt_small(VAR, VAR, GN_EPS)
                tt("dve", DD, YSB, PS_GNM, ALU.subtract)
                tt("dve", DD, DD, VAR, ALU.mult)
                ts("dve", DD, DD, gng[:, c:c + 1], gnb[:, c:c + 1], ALU.mult, ALU.add)
                sigm(M2, DD, pool=True)
                tt("dve", sub(MIXT, (slice(None), c, slice(None)), c), DD, M2, ALU.mult)
            st.append(gn)
        return st

    def steps_D(m):
        st = []
        HT = HT2[m % 2]
        QS = QS2[m % 2]
        MIXT = MIXT2[m % 2]
        cf3 = V(PS_CF.ap.rearrange("p (h t) -> p h t", h=4), PS_CF.bufs)
        ct3 = V(PS_CT.ap.rearrange("p (h v) -> p h v", h=4), PS_CT.bufs)
        mk3 = V(MASK4.ap.rearrange("p (h t) -> p h t", h=4), MASK4.bufs)
        for s_ in range(NSUB):
            hts = V(HT.ap[:, :, s_ * 128:(s_ + 1) * 128], [HT.bufs[s_]])

            def tok_proj(ps, grp, hts=hts):
                w = sub(WIN, (slice(None), slice(None), slice(1536 + grp * 512, 2048 + grp * 512)), 3 + grp)
                mm(ps, onesb_row, BINT[:, grp * 512:(grp + 1) * 512], True, False)
                for k in range(8):
                    mm(ps, hts[:, k, :], w[:, k, :], False, k == 7)

            def d1(tok_proj=tok_proj):
                tok_proj(PS_TMA, 0)
                sigm(A1, PS_TMA, pos=False)
                tt("dve", KK, A1, OMLB, ALU.mult)
                act(LF, KK, AF.Ln, bias=1.0, scale=-1.0)

            def d2(tok_proj=tok_proj):
                tok_proj(PS_TMB, 1)
                cp("act", VB, PS_TMB)

            def d2b(tok_proj=tok_proj):
                tok_proj(PS_TMA, 2)
                sigm(GF, PS_TMA, pool=True)
                tt("dve", GF, PS_TMA, GF, ALU.mult)
                tt("dve", GF, GF, GOUT, ALU.mult)

            def d3():
                mm(PS_CT, CC, LF, True, True)
                for hd in range(4):
                    mm(PS_CF[:, hd * 128:(hd + 1) * 128], LF[:, hd * 128:(hd + 1) * 128], TRI, True, True)
                act(A1, PS_CT, AF.Exp, scale=-1.0)
                tt("dve", KT, KK, A1, ALU.mult)
                ts("dve", NEGMID, cf3[:, :, 63], -1.0, None, ALU.mult)
                act(EM, cf3[:, :, 63], AF.Exp, scale=1.0)
                act(EL, cf3[:, :, 127], AF.Exp, scale=1.0)
                for hd in range(4):
                    act(EP[:, hd, :], cf3[:, hd, :], AF.Exp, bias=NEGMID[:, hd:hd + 1], scale=1.0)

            def d4(s_=s_):
                for hd in range(4):
                    tr(PS_CTB[:, hd * 128:(hd + 1) * 128], KT[:, hd * 128:(hd + 1) * 128], IDB)
                cp("act", KTT, V(PS_CTB.ap[:, 0:512].rearrange("p (h t) -> p h t", h=4), PS_CTB.bufs))
                qs_s = V(QS.ap[:, :, s_ * 128:(s_ + 1) * 128], QS.bufs)
                tt("dve", QT, qs_s, EP, ALU.mult)
                for hd in range(4):
                    ts("dve", SM[:, hd, :], STATE[:, hd, :], EM[:, hd:hd + 1], None, ALU.mult)

            def d5():
                for hd in range(4):
                    mm(PS_CF[0:64, hd * 128:hd * 128 + 64], KTT[:, hd, 0:64], QT[:, hd, 0:64], True, True)
                    mm(PS_CF[:, hd * 128 + 64:hd * 128 + 128], KTT[:, hd, :], QT[:, hd, 64:128], True, True)
                tt("dve", ATM[0:64, :, 0:64], cf3[0:64, :, 0:64], mk3[0:64, :, 0:64], ALU.mult)
                tt("dve", ATM[:, :, 64:128], cf3[:, :, 64:128], mk3[:, :, 64:128], ALU.mult)
                for hd in range(4):
                    mm(PS_CT[:, hd * 128:(hd + 1) * 128], KT[:, hd * 128:(hd + 1) * 128], VB[:, hd * 128:(hd + 1) * 128], True, True)
                for hd in range(4):
                    act(TMPS[:, hd, :], ct3[:, hd, :], AF.Copy, scale=EP[:, hd, 127:128])

            def d6():
                for hd in range(4):
                    o_hd = PS_O[:, hd * 128:(hd + 1) * 128]
                    mm(o_hd, ATM[:, hd, :], VB[:, hd * 128:(hd + 1) * 128], True, False)
                    mm(o_hd, QT[:, hd, :], SM[:, hd, :], False, True)
                for hd in range(4):
                    act(HG[:, hd * 128:(hd + 1) * 128], PS_O[:, hd * 128:(hd + 1) * 128], AF.Square,
                        scale=float(1.0 / np.sqrt(128.0)), accum=SS4[:, hd:hd + 1])
                rsqrt_small(RS4, SS4, RMS_EPS)
                for hd in range(4):
                    stt("dve", HG[:, hd * 128:(hd + 1) * 128], PS_O[:, hd * 128:(hd + 1) * 128], RS4[:, hd:hd + 1],
                        GF[:, hd * 128:(hd + 1) * 128], ALU.mult, ALU.mult)
                for hd in range(4):
                    stt("dve", STATE[:, hd, :], STATE[:, hd, :], EL[:, hd:hd + 1], TMPS[:, hd, :], ALU.mult, ALU.add)

            def d7(s_=s_):
                for hd in range(4):
                    tr(PS_TMBB[:, hd * 128:(hd + 1) * 128], HG[:, hd * 128:(hd + 1) * 128], IDB)
                dst = V(MIXT.ap[:, 4:8, s_ * 128:(s_ + 1) * 128], [MIXT.bufs[4 + s_]])
                cp("act", dst, V(PS_TMBB.ap[:, 0:512].rearrange("p (h t) -> p h t", h=4), PS_TMBB.bufs))
            st.extend([d1, d2, d2b, d3, d4, d5, d6, d7])
        return st

    def steps_E(m):
        st = []
        MIXT = MIXT2[m % 2]
        for s_ in range(NSUB):
            lh = V(MIXT.ap[:, :, s_ * 128:(s_ + 1) * 128], list(MIXT.bufs[0:4]) + [MIXT.bufs[4 + s_]])

            def e1(lh=lh, hf=0):
                wo = sub(WOUT, (slice(None), slice(None), slice(hf * 512, (hf + 1) * 512)), hf)
                for k in range(8):
                    mm(PS_Y[:, hf * 512:(hf + 1) * 512], lh[:, k, :], wo[:, k, :], k == 0, k == 7)

            def e2(s_=s_):
                post_stage(PS_Y, xs_of(m)[s_], GT1, HFP, HBP, STAT2)
                ti = m * NSUB + s_
                P.dma("sp", V(x1_d[ti * 128:(ti + 1) * 128, :], [x1_bufs[ti]]), xs_of(m)[s_], key="st%d" % (ti % 4))
            st.append(lambda lh=lh: e1(lh, 0))
            st.append(lambda lh=lh: e1(lh, 1))
            st.append(e2)
        return st

    def run_interleaved(xs_, ys_):
        nx, ny = len(xs_), len(ys_)
        ix = iy = 0
        while ix < nx or iy < ny:
            if iy >= ny or (ix < nx and ix * max(ny, 1) <= iy * max(nx, 1)):
                xs_[ix]()
                ix += 1
            else:
                ys_[iy]()
                iy += 1

    for f_ in steps_A(0) + steps_B(0) + steps_C(0):
        f_()
    for m in range(NM):
        xstream = steps_D(m) + steps_E(m)
        ystream = (steps_A(m + 1) + steps_B(m + 1) + steps_C(m + 1)) if m + 1 < NM else []
        if PH1_MODEL:
            P.interleave([xstream, ystream])
        else:
            run_interleaved(xstream, ystream)
        chk(7 + 100 * m, [xs_of(m)[0], xs_of(m)[1]])

    chk(6, [XS[0]])
    fence("dve", R_ph1w.bufs + R_ph1t.bufs, R_ph2.bufs)
    for j in range(8):
        P.dma("pool", sub(WUP, (slice(None), slice(None), slice(j * 512, (j + 1) * 512)), j),
              wup_v[:, :, j * 512:(j + 1) * 512], key="wup%d" % j)
    P.dma("sp", G2, V(modf_d[:, 0:1024], [modf_buf]), key="mf")
    P.dma("sp", SH2, V(modf_d[:, 1024:2048], [modf_buf]), key="mf")
    P.dma("sp", GT2, V(modf_d[:, 2048:3072], [modf_buf]), key="mf")
    for j in range(8):
        P.dma("pool", sub(WDN, (slice(None), slice(4 * j, 4 * j + 4), slice(None)), j),
              wdn_v[:, 4 * j:4 * j + 4, :], key="wdn%d" % j)

    def load_x1(m):
        for s_ in range(NSUB):
            slot = (m % 2) * NSUB + s_
            ti = m * NSUB + s_
            P.dma("sp", XS[slot], V(x1_d[ti * 128:(ti + 1) * 128, :], [x1_bufs[ti]]), key=xkeys[slot])

    PS_UP = [pv(1, 0, 256), pv(2, 0, 256), pv(3, 0, 256)]
    PS_Y2 = [V(psum[:, 4 * 512:6 * 512], [bankbuf[4], bankbuf[5]]), V(psum[:, 6 * 512:8 * 512], [bankbuf[6], bankbuf[7]])]

    def steps_N2(m):
        st = [lambda: load_x1(m)]
        for s_ in range(NSUB):
            st.append(lambda s_=s_: norm_a(xs_of(m)[s_], G2, SH2))
            st.append(lambda s_=s_: norm_b(H2T2[m % 2], s_))
        return st

    def steps_UP(m):
        st = []
        H2T = H2T2[m % 2]
        for f in range(32):
            def up(f=f):
                ps = PS_UP[f % 3]
                wu = sub(WUP, (slice(None), slice(None), slice(f * 128, (f + 1) * 128)), f // 4)
                for k in range(8):
                    mm(ps, wu[:, k, :], H2T[:, k, :], k == 0, k == 7)
                rr = RR[f % 3]
                act(rr, ps, AF.Relu)
                tt("pool", sub(AT, (slice(None), f, slice(None)), f), rr, rr, ALU.mult)
            st.append(up)
        return st

    def steps_DOWN(m):
        st = []
        for f in range(32):
            def dn(f=f):
                for s_ in range(NSUB):
                    for hf in range(2):
                        wd = sub(WDN, (slice(None), f, slice(hf * 512, (hf + 1) * 512)), f // 4)
                        lh = V(AT.ap[:, f, s_ * 128:(s_ + 1) * 128], [AT.bufs[f]])
                        mm(PS_Y2[s_][:, hf * 512:(hf + 1) * 512], lh, wd, f == 0, f == 31)
            st.append(dn)
        for s_ in range(NSUB):
            def po(s_=s_):
                post_stage(PS_Y2[s_], xs_of(m)[s_], GT2, PS_Y2[s_], PS_TR, STAT3)
                ti = m * NSUB + s_
                P.dma("sp", out_d[ti * 128:(ti + 1) * 128, :], xs_of(m)[s_], key="st%d" % (ti % 4), is_output=True)
            st.append(po)
        return st

    for f_ in steps_N2(0):
        f_()
    for m in range(NM):
        streams = [steps_DOWN(m), steps_UP(m)]
        if m + 1 < NM:
            streams.append(steps_N2(m + 1))
        P.interleave(streams, ready=lambda i, pos: not (i == 0 and pos[0] < 32 and pos[1] <= pos[0]))

    P.emit()
    return nc, P


def _consts():
    s = np.arange(128)[:, None]
    t = np.arange(128)[None, :]
    ident = (s == t).astype(np.float32)
    tri = (s <= t).astype(np.float32)
    cc = np.zeros((128, 128), np.float32)
    cc[(s >= 64) & (s <= t)] = 1.0
    cc[(s <= 63) & (s > t)] = -1.0
    gm = ((s // 64) == (t // 64)).astype(np.float32) / 64.0
    ones = np.ones((128, 128), np.float32)
    mask4 = np.tile(tri, (1, 4))
    return np.ascontiguousarray(np.concatenate([ident, tri, cc, gm, ones, mask4], axis=1))


_NC_CACHE = {}


def run(inputs, S=None, stop=0):
    x = np.asarray(inputs["x"], np.float32)
    B, S_, _ = x.shape
    S = S_ if S is None else S
    f = lambda k: np.asarray(inputs[k], np.float32)
    c = f("c")
    b_in = f("b_in")[0]
    w_dw = f("w_dw")[0]
    lbl = f("lb_logits")
    rowpack = np.concatenate([f("g_pre_mix")[0], f("g_post_mix")[0], f("g_pre_ffn")[0], f("g_post_ffn")[0],
                              lbl[0], lbl[1], f("g_hgrn_out")[0]])[None, :]
    col_common = np.concatenate([
        b_in[:1536].reshape(12, 128).T,
        w_dw.T.reshape(4, 128, 31).transpose(1, 0, 2).reshape(128, 124),
        f("b_dw")[0].reshape(4, 128).T, f("gn_gain")[0].reshape(4, 128).T, f("gn_bias")[0].reshape(4, 128).T], axis=1)
    shared = {
        "w_ada": np.ascontiguousarray(f("w_ada")[0]), "w_in": np.ascontiguousarray(f("w_in")[0]),
        "w_out": np.ascontiguousarray(f("w_out")[0]), "w_up": np.ascontiguousarray(f("w_up")[0]),
        "w_down": np.ascontiguousarray(f("w_down")[0]),
        "rowpack": np.ascontiguousarray(rowpack), "b_ada_row": np.ascontiguousarray(f("b_ada")),
        "b_in_tok": np.ascontiguousarray(b_in[None, 1536:]), "consts": _consts(),
    }
    in_maps = []
    for b in range(B):
        m = dict(shared)
        m["x"] = np.ascontiguousarray(x[b])
        m["colpack"] = np.ascontiguousarray(np.concatenate([c[b].reshape(8, 128).T, col_common], axis=1))
        in_maps.append(m)
    if (S, stop) not in _NC_CACHE:
        _NC_CACHE[(S, stop)] = build_nc(S, stop)[0]
    res = run_bass_kernel_spmd(_NC_CACHE[(S, stop)], in_maps, core_ids=list(range(B)))
    return np.stack([np.asarray(r["out"], np.float32) for r in res.results], axis=0)


def kernel(**inputs):
    return run(inputs)
```

```python
import numpy as np
import concourse.bass as bass
import concourse.mybir as mybir
from concourse.bass_utils import run_bass_kernel_spmd

F32 = mybir.dt.float32
BF16 = mybir.dt.bfloat16
AF = mybir.ActivationFunctionType
ALU = mybir.AluOpType

D = 1024
NCORES = 8
SEQ = 4096
T = 256
NSUB = T // 128
RMS_EPS = 1e-6
PH1_MODEL = True
ATTACH_WAITS = True
SIGM_DVE = False
SIGM_POOL = False
GN_EPS = 1e-5
ENGS = ("pe", "act", "dve", "pool", "sp")


class Buf:
    __slots__ = ("name", "lastw", "readers")

    def __init__(self, name):
        self.name = name
        self.lastw = None
        self.readers = []


class V:
    __slots__ = ("ap", "bufs")

    def __init__(self, ap, bufs):
        self.ap = ap
        self.bufs = tuple(bufs)

    def __getitem__(self, idx):
        return V(self.ap[idx], self.bufs)


class Op:
    __slots__ = ("eng", "fn", "raw", "war", "idx", "inc", "semval", "dma_key", "dma_val", "waits", "t_end")

    def __init__(self, eng, fn):
        self.eng = eng
        self.fn = fn
        self.raw = []
        self.war = []
        self.idx = -1
        self.inc = False
        self.semval = 0
        self.dma_key = None
        self.dma_val = 0
        self.waits = []
        self.t_end = 0.0


class Prog:
    def __init__(self, nc):
        self.nc = nc
        self.eng_ops = {e: [] for e in ENGS}
        self.all_ops = []
        self.dma_count = {}
        self.dma_last = {}
        self.final_waits = []
        self.eng_free = {e: 0.0 for e in ENGS}
        self.log = None

    def _add(self, eng, fn, reads, writes, dma_key=None, cost=100.0):
        op = Op(eng, fn)
        rb, wb = [], []
        for v in reads:
            if v is None or isinstance(v, (int, float)):
                continue
            for b in (v.bufs if isinstance(v, V) else (v,)):
                if b not in rb:
                    rb.append(b)
        for v in writes:
            for b in (v.bufs if isinstance(v, V) else (v,)):
                if b not in wb:
                    wb.append(b)
        for b in rb:
            if b.lastw is not None and b.lastw not in op.raw:
                op.raw.append(b.lastw)
        for b in wb:
            if b.lastw is not None and b.lastw not in op.raw:
                op.raw.append(b.lastw)
            for r in b.readers:
                if r is not op and r not in op.war and r not in op.raw:
                    op.war.append(r)
        log = self.log
        for b in rb:
            if log is not None:
                log.append((0, b, len(b.readers)))
            b.readers.append(op)
        for b in wb:
            if log is not None:
                log.append((1, b, b.lastw, b.readers))
            b.lastw = op
            b.readers = []
        op.idx = len(self.eng_ops[eng])
        self.eng_ops[eng].append(op)
        self.all_ops.append(op)
        if dma_key is not None:
            op.dma_key = dma_key
            if log is not None:
                log.append((2, dma_key, self.dma_count.get(dma_key), self.dma_last.get(dma_key)))
            self.dma_count[dma_key] = self.dma_count.get(dma_key, 0) + 16
            op.dma_val = self.dma_count[dma_key]
            prev = self.dma_last.get(dma_key)
            if prev is not None and prev not in op.raw:
                op.raw.append(prev)
            self.dma_last[dma_key] = op
        t0 = self.eng_free[eng]
        for p in op.raw:
            lat = 60.0 if (p.eng == eng and p.dma_key is None) else 180.0
            if not (p.eng == eng == "pe"):
                t0 = max(t0, p.t_end + lat)
        for p in op.war:
            if p.eng != eng or p.dma_key is not None:
                t0 = max(t0, p.t_end + 180.0)
        if dma_key is not None:
            self.eng_free[eng] = t0 + 60.0
            op.t_end = t0 + cost
        else:
            op.t_end = t0 + cost
            self.eng_free[eng] = op.t_end
        return op

    def checkpoint(self):
        assert self.log is None
        self.log = []
        return ({e: len(self.eng_ops[e]) for e in ENGS}, dict(self.eng_free), len(self.final_waits), len(self.all_ops))

    def rollback(self, cp_):
        lens, free, nfw, nall = cp_
        for ent in reversed(self.log):
            if ent[0] == 0:
                del ent[1].readers[ent[2]:]
            elif ent[0] == 1:
                ent[1].lastw = ent[2]
                ent[1].readers = ent[3]
            else:
                k = ent[1]
                if ent[2] is None:
                    self.dma_count.pop(k, None)
                    self.dma_last.pop(k, None)
                else:
                    self.dma_count[k] = ent[2]
                    self.dma_last[k] = ent[3]
        for e in ENGS:
            del self.eng_ops[e][lens[e]:]
        del self.all_ops[nall:]
        self.eng_free = free
        del self.final_waits[nfw:]
        self.log = None

    def commit(self):
        self.log = None

    def interleave(self, streams, ready=None):
        pos = [0] * len(streams)
        while True:
            alive = [i for i in range(len(streams)) if pos[i] < len(streams[i])]
            if not alive:
                break
            cand = [i for i in alive if ready is None or ready(i, pos)]
            assert cand, "interleave: no stream is ready"
            if len(cand) == 1:
                i = cand[0]
                streams[i][pos[i]]()
                pos[i] += 1
                continue
            best, best_m = None, None
            for i in cand:
                pe0 = self.eng_free["pe"]
                n0 = len(self.eng_ops["pe"])
                cp_ = self.checkpoint()
                streams[i][pos[i]]()
                work = 0.0
                tprev = pe0
                idle = 0.0
                for op in self.eng_ops["pe"][n0:]:
                    idle += 0.0
                pe1 = self.eng_free["pe"]
                work = sum(op.semval for op in self.eng_ops["pe"][n0:])
                metric = (pe1 - pe0) - work
                if n0 == len(self.eng_ops["pe"]):
                    metric = -1.0
                self.rollback(cp_)
                if best is None or metric < best_m - 1.0:
                    best, best_m = i, metric
            streams[best][pos[best]]()
            pos[best] += 1

    def op(self, eng, fn, reads=(), writes=(), cost=100.0):
        o = self._add(eng, fn, reads, writes, cost=cost)
        o.semval = cost
        return o

    def dma(self, eng, out, in_, key, is_output=False):
        o_ap = out.ap if isinstance(out, V) else out
        i_ap = in_.ap if isinstance(in_, V) else in_
        reads = [in_] if isinstance(in_, V) else []
        writes = [out] if isinstance(out, V) else []
        nbytes = 1
        for d_ in o_ap.shape:
            nbytes *= int(d_)
        nbytes *= 4
        op = self._add(eng, lambda e: e.dma_start(out=o_ap, in_=i_ap), reads, writes, dma_key=key,
                       cost=2000.0 + nbytes / 150.0)
        if is_output:
            self.final_waits.append(op)
        return op

    def emit(self):
        nc = self.nc
        know = {e: {} for e in ENGS}
        opknow = {}
        for op in self.all_ops:
            e = op.eng
            ke = know[e]
            need = {}
            for is_war, plist in ((False, op.raw), (True, op.war)):
                for p in plist:
                    if p.dma_key is not None:
                        k_, v_ = "d:" + str(p.dma_key), p.dma_val
                    else:
                        if p.eng == e and e == "pe":
                            continue
                        k_, v_ = p.eng, p.idx
                    if ke.get(k_, -1) < v_ and need.get(k_, (-1, None))[0] < v_:
                        need[k_] = (v_, p)
            op.waits = []
            for k_, (v_, p) in need.items():
                if ke.get(k_, -1) >= v_:
                    continue
                if p.dma_key is not None:
                    op.waits.append(("dma", p.dma_key, v_))
                else:
                    p.inc = True
                    op.waits.append(("eng", p.eng, p))
                ke[k_] = max(ke.get(k_, -1), v_)
                for kk_, vv_ in opknow[id(p)].items():
                    if ke.get(kk_, -1) < vv_:
                        ke[kk_] = vv_
            kn = dict(ke)
            if op.dma_key is None:
                kn[e] = max(kn.get(e, -1), op.idx)
            opknow[id(op)] = kn
        seen_dma = {"sp": {k_[2:]: v_ for k_, v_ in know["sp"].items() if isinstance(k_, str) and k_.startswith("d:")}}
        fw = {}
        for p in self.final_waits:
            if seen_dma["sp"].get(p.dma_key, 0) < p.dma_val:
                fw[p.dma_key] = max(fw.get(p.dma_key, 0), p.dma_val)
        for e in ENGS:
            c = 0
            for op in self.eng_ops[e]:
                if op.inc:
                    c += 1
                    op.semval = c
        esem = {e: nc.alloc_semaphore(name="sem_" + e) for e in ENGS}
        dsem = {k: nc.alloc_semaphore(name="dsem_" + str(k)) for k in self.dma_count}
        handles = {"pe": "tensor", "act": "scalar", "dve": "vector", "pool": "gpsimd", "sp": "sync"}
        self.n_waits = 0
        self.n_sems = len(esem) + len(dsem)

        def run_engine(e, eng):
            for op in self.eng_ops[e]:
                ws = op.waits
                sep = ws[:-1] if ATTACH_WAITS else ws
                for w in sep:
                    self.n_waits += 1
                    if w[0] == "eng":
                        eng.wait_ge(esem[w[1]], w[2].semval)
                    else:
                        eng.wait_ge(dsem[w[1]], w[2])
                ins = op.fn(eng)
                if ATTACH_WAITS and ws:
                    w = ws[-1]
                    if w[0] == "eng":
                        ins = ins._wait_ge(esem[w[1]], w[2].semval)
                    else:
                        ins = ins._wait_ge(dsem[w[1]], w[2])
                if op.dma_key is not None:
                    ins.then_inc(dsem[op.dma_key], 16)
                elif op.inc:
                    ins.then_inc(esem[e], 1)
            if e == "sp":
                for k, v in fw.items():
                    eng.wait_ge(dsem[k], v)

        with nc.Block() as block:
            for e in ENGS:
                deco = getattr(block, handles[e])

                def body(eng, e=e):
                    run_engine(e, eng)

                deco(body)


NC_COL = 8 + 12 + 124 + 12
NR_ROW = 4096 + 1024 + 512
NK_CONST = 128 * 5 + 512


class _Stop(Exception):
    pass


def build_nc(S, stop=0):
    holder = {}
    try:
        return _build(S, stop, holder)
    except _Stop:
        return holder["nc"], holder["P"]


def _build(S, stop, holder):
    assert S % T == 0
    NM = S // T
    nc = bass.Bass("TRN2", target_bir_lowering=False)
    P = Prog(nc)
    holder["nc"] = nc
    holder["P"] = P

    x_d = nc.dram_tensor("x", [S, D], F32, kind="ExternalInput").ap()
    wada_d = nc.dram_tensor("w_ada", [D, 6 * D], F32, kind="ExternalInput").ap()
    win_d = nc.dram_tensor("w_in", [D, 3072], F32, kind="ExternalInput").ap()
    wout_d = nc.dram_tensor("w_out", [D, D], F32, kind="ExternalInput").ap()
    wup_d = nc.dram_tensor("w_up", [D, 4 * D], F32, kind="ExternalInput").ap()
    wdn_d = nc.dram_tensor("w_down", [4 * D, D], F32, kind="ExternalInput").ap()
    col_d = nc.dram_tensor("colpack", [128, NC_COL], F32, kind="ExternalInput").ap()
    row_d = nc.dram_tensor("rowpack", [1, NR_ROW], F32, kind="ExternalInput").ap()
    bada_d = nc.dram_tensor("b_ada_row", [1, 6 * D], F32, kind="ExternalInput").ap()
    bint_d = nc.dram_tensor("b_in_tok", [1, 1536], F32, kind="ExternalInput").ap()
    cst_d = nc.dram_tensor("consts", [128, NK_CONST], F32, kind="ExternalInput").ap()
    out_d = nc.dram_tensor("out", [S, D], F32, kind="ExternalOutput").ap()
    x1_d = nc.dram_tensor("x1_scratch", [S, D], F32).ap()
    modf_d = nc.dram_tensor("modf_scratch", [128, 3 * D], F32).ap()
    x1_bufs = [Buf("x1d%d" % i) for i in range(S // 128)]
    modf_buf = Buf("modfd")

    ARENA = 53200
    arena = nc.alloc_sbuf_tensor("arena", [128, ARENA], F32).ap()

    class Region:
        def __init__(self, start, end):
            self.start, self.end, self.ptr = start, end, start
            self.bufs = []

    def alloc(reg, nbytes, dt=F32, name="t", parts=128, shape=None, nbuf=1):
        words = (nbytes + 3) // 4
        off = reg.ptr
        reg.ptr += words
        assert reg.ptr <= reg.end, ("arena region overflow", name, reg.ptr, reg.end)
        ap = arena[0:parts, off:off + words]
        if dt != F32:
            ap = ap.bitcast(dt)
        if shape is not None:
            ap = ap.rearrange(shape[0], **shape[1])
        bufs = [Buf(name + str(i)) for i in range(nbuf)]
        reg.bufs.extend(bufs)
        return V(ap, bufs)

    def sub(v, idx, bi):
        return V(v.ap[idx], [v.bufs[bi]])

    COMMON_W = 10170
    R_common = Region(0, COMMON_W)
    R_ph1w = Region(COMMON_W, COMMON_W + 26700)
    R_ph1t = Region(COMMON_W + 26700, ARENA)
    R_st = Region(COMMON_W + 26700, ARENA)
    R_ph2 = Region(COMMON_W, ARENA)

    CST = alloc(R_common, NK_CONST * 4, name="cst")
    IDENT = CST[:, 0:128]
    TRI = CST[:, 128:256]
    CC = CST[:, 256:384]
    GM = CST[:, 384:512]
    ONES = CST[:, 512:640]
    MASK4 = CST[:, 640:1152]
    IDB = alloc(R_common, 256, BF16, "idb")
    ONESB = alloc(R_common, 256, BF16, "onesb")
    COL = alloc(R_common, NC_COL * 4, name="col")
    NEGB = alloc(R_common, 12 * 4, name="negb")
    G1 = alloc(R_common, 4096, name="G1")
    SH1 = alloc(R_common, 4096, name="SH1")
    GT1 = alloc(R_common, 4096, name="GT1")
    XS = [alloc(R_common, 4096, name="X%d" % i) for i in range(4)]
    HF = alloc(R_common, 4096, name="HF")
    HB = alloc(R_common, 2048, BF16, name="HB")
    STAT = alloc(R_common, 64, name="stat")
    c_col = COL[:, 0:8]
    bfm = COL[:, 8:20]
    wdwT = COL[:, 20:144]
    bdw = COL[:, 144:148]
    gng = COL[:, 148:152]
    gnb = COL[:, 152:156]

    WIN = alloc(R_ph1w, 8 * 3072 * 2, BF16, "win", shape=("p (k n) -> p k n", dict(k=8)), nbuf=6)
    WOUT = alloc(R_ph1w, 8 * 1024 * 2, BF16, "wout", shape=("p (k n) -> p k n", dict(k=8)), nbuf=2)
    DG = alloc(R_ph1w, 124 * 128 * 2, BF16, "dg", shape=("p (c m) -> p c m", dict(c=124)), nbuf=4)
    OMLB = alloc(R_ph1w, 2048, name="omlb")
    GOUT = alloc(R_ph1w, 2048, name="gout")
    BINT = alloc(R_ph1w, 1536 * 2, BF16, "bint", parts=1)
    STATE = alloc(R_ph1w, 2048, name="state", shape=("p (h v) -> p h v", dict(h=4)))

    HT2 = [alloc(R_ph1t, 8 * T * 2, BF16, "hT%d" % i, shape=("p (k t) -> p k t", dict(k=8)), nbuf=NSUB) for i in range(2)]
    HFP = alloc(R_ph1t, 4096, name="hfp")
    HBP = alloc(R_ph1t, 2048, BF16, name="hbp")
    STAT2 = alloc(R_ph1t, 64, name="stat2")
    UE = [alloc(R_ph1t, 4 * (30 + T) * 2, BF16, "ue%d" % i, shape=("p (c t) -> p c t", dict(c=4)), nbuf=4)
          for i in range(2)]
    TA = [alloc(R_ph1t, T * 4, name="ta%d" % i) for i in range(2)]
    QS2 = [alloc(R_ph1t, 4 * T * 4, name="qs%d" % i, shape=("p (h t) -> p h t", dict(h=4)), nbuf=4) for i in range(2)]
    YSB = alloc(R_ph1t, T * 4, name="ysb")
    Y2SB = alloc(R_ph1t, T * 4, name="y2sb")
    M2 = alloc(R_ph1t, T * 4, name="m2")
    VAR = alloc(R_ph1t, T * 4, name="var")
    DD = alloc(R_ph1t, T * 4, name="dd")
    MIXT2 = [alloc(R_ph1t, 8 * T * 2, BF16, "mixT%d" % i, shape=("p (k t) -> p k t", dict(k=8)), nbuf=4 + NSUB) for i in range(2)]
    A1 = alloc(R_ph1t, 2048, name="a1")
    KK = alloc(R_ph1t, 2048, name="kk")
    LF = alloc(R_ph1t, 2048, name="lf")
    KT = alloc(R_ph1t, 1024, BF16, "kt")
    KTT = alloc(R_ph1t, 1024, BF16, "ktt", shape=("p (h t) -> p h t", dict(h=4)))
    VB = alloc(R_ph1t, 1024, BF16, "vb")
    SMALL = alloc(R_ph1t, 6 * 16, name="small")
    EP = alloc(R_ph1t, 2048, name="ep", shape=("p (h t) -> p h t", dict(h=4)))
    QT = alloc(R_ph1t, 1024, BF16, "qt", shape=("p (h t) -> p h t", dict(h=4)))
    ATM = alloc(R_ph1t, 1024, BF16, "atm", shape=("p (h t) -> p h t", dict(h=4)))
    SM = alloc(R_ph1t, 1024, BF16, "sm", shape=("p (h v) -> p h v", dict(h=4)))
    TMPS = V(KK.ap.rearrange("p (h v) -> p h v", h=4), KK.bufs)
    T1 = LF
    SG = A1
    HG = alloc(R_ph1t, 1024, BF16, "hg")
    GF = alloc(R_ph1t, 2048, name="gf")
    ONES512 = alloc(R_ph1t, 2048, name="ones512")
    NEGONES512 = alloc(R_ph1t, 2048, name="negones512")

    BADA = alloc(R_st, 6 * D * 2, BF16, "bada", parts=1)
    ROW = alloc(R_st, NR_ROW * 4, name="row", parts=1)
    CB = alloc(R_st, 8 * 128 * 2, BF16, "cb", shape=("p (k m) -> p k m", dict(k=8)))
    CACT = alloc(R_st, 64, name="cact")
    LBT = alloc(R_st, 2048, name="lbt", parts=1)
    WA = [alloc(R_st, 8 * 512 * 2, BF16, "wa%d" % i, shape=("p (k n) -> p k n", dict(k=8))) for i in range(3)]

    WUP = alloc(R_ph2, 8 * 4096 * 2, BF16, "wup", shape=("p (k n) -> p k n", dict(k=8)), nbuf=8)
    WDN = alloc(R_ph2, 32 * 1024 * 2, BF16, "wdn", shape=("p (k n) -> p k n", dict(k=32)), nbuf=8)
    G2 = alloc(R_ph2, 4096, name="G2")
    SH2 = alloc(R_ph2, 4096, name="SH2")
    GT2 = alloc(R_ph2, 4096, name="GT2")
    H2T2 = [alloc(R_ph2, 8 * T * 2, BF16, "h2T%d" % i, shape=("p (k t) -> p k t", dict(k=8)), nbuf=NSUB) for i in range(2)]
    STAT3 = alloc(R_ph2, 64, name="stat3")
    AT = alloc(R_ph2, 32 * T * 2, BF16, "aT", shape=("p (f t) -> p f t", dict(f=32)), nbuf=32)
    RR = [alloc(R_ph2, 2 * T * 4, name="rr%d" % i) for i in range(2)]

    psum = nc.alloc_psum_tensor("psum", [128, 4096], F32).ap()

    bankbuf = [Buf("bank%d" % i) for i in range(8)]

    def pv(b, lo=0, hi=512):
        return V(psum[:, b * 512 + lo:b * 512 + hi], [bankbuf[b]])

    PS_TR = pv(0)
    PS_TRB = V(psum[:, 0:512].bitcast(BF16), PS_TR.bufs)
    PS_GNM = pv(0, 0, 256)
    PS_GNE = pv(0, 256, 512)
    PS_FMA = pv(1, 0, 256)
    PS_FMB = pv(2, 0, 256)
    PS_CVA = pv(1, 0, 256)
    PS_CVB = pv(2, 0, 256)
    PS_TMA = pv(3)
    PS_TMB = pv(4)
    PS_TMBB = V(psum[:, 4 * 512:5 * 512].bitcast(BF16), PS_TMB.bufs)
    PS_Y = V(psum[:, 3 * 512:5 * 512], [bankbuf[3], bankbuf[4]])
    PS_CT = pv(5)
    PS_CTB = V(psum[:, 5 * 512:6 * 512].bitcast(BF16), PS_CT.bufs)
    PS_CF = pv(6)
    PS_O = pv(7)

    def apof(v):
        return v.ap if isinstance(v, V) else v

    def fsz(v):
        n = 1
        for d_ in v.ap.shape[1:]:
            n *= int(d_)
        return n

    def is_ps(v):
        return isinstance(v, V) and v.bufs and v.bufs[0].name.startswith("bank")

    def mm(out, lhsT, rhs, start, stop):
        n = max(64, fsz(rhs)) * (4 if rhs.ap.dtype == F32 else 1)
        P.op("pe", lambda e: e.matmul(out.ap, lhsT=lhsT.ap, rhs=rhs.ap, start=start, stop=stop), [lhsT, rhs], [out],
             cost=n / 1.95 + 15.0)

    def tr(out, in_, ident):
        P.op("pe", lambda e: e.transpose(out.ap, in_.ap, ident.ap), [in_, ident], [out], cost=110.0)

    def act(out, in_, func, bias=None, scale=None, accum=None):
        kw = {}
        if bias is not None:
            kw["bias"] = apof(bias)
        if scale is not None:
            kw["scale"] = apof(scale)
        if accum is not None:
            kw["accum_out"] = accum.ap
        w = [out] + ([accum] if accum is not None else [])
        P.op("act", lambda e: e.activation(out=out.ap, in_=in_.ap, func=func, **kw), [in_, bias, scale], w,
             cost=(224.0 + fsz(out)) / 1.2)

    def sigm(out, in_, negbias=None, pos=True, pool=False):
        if pos:
            act(out, in_, AF.Exp, bias=negbias, scale=-1.0)
        else:
            assert negbias is None
            act(out, in_, AF.Exp, scale=1.0)
        if pool and SIGM_POOL:
            n_ = fsz(out)
            tt("pool", out, out, ONES512[:, 0:n_], ALU.add)
            tt("pool", out, out, NEGONES512[:, 0:n_], ALU.pow)
        elif SIGM_DVE:
            ts("dve", out, out, 1.0, None, ALU.add)
            P.op("dve", lambda e: e.reciprocal(out=out.ap, in_=out.ap), [out], [out], cost=(60.0 + fsz(out)) / 0.96)
        else:
            act(out, out, AF.Ln, bias=1.0, scale=1.0)
            act(out, out, AF.Exp, scale=-1.0)

    def rsqrt_small(out, in_, eps):
        act(out, in_, AF.Ln, bias=eps, scale=1.0)
        act(out, out, AF.Exp, scale=-0.5)

    def vcost(eng, out, ins, fast=False):
        f = fsz(out)
        if eng == "pool":
            return 100.0 + 2.2 * f
        base = 120.0 if any(is_ps(v) for v in ins) else 60.0
        if fast and base == 60.0:
            f = f / 2.0
        return (base + f) / 0.96

    def tt(eng, out, in0, in1, op):
        P.op(eng, lambda e: e.tensor_tensor(out=out.ap, in0=in0.ap, in1=in1.ap, op=op), [in0, in1], [out],
             cost=vcost(eng, out, [in0, in1]))

    def stt(eng, out, in0, scalar, in1, op0, op1):
        P.op(eng, lambda e: e.scalar_tensor_tensor(out=out.ap, in0=in0.ap, scalar=apof(scalar), in1=in1.ap,
                                                    op0=op0, op1=op1), [in0, scalar, in1], [out],
             cost=vcost(eng, out, [in0, in1]))

    def ts(eng, out, in0, s1, s2, op0, op1=None):
        if op1 is None:
            P.op(eng, lambda e: e.tensor_scalar(out=out.ap, in0=in0.ap, scalar1=apof(s1), scalar2=None, op0=op0),
                 [in0, s1], [out], cost=vcost(eng, out, [in0], True))
        else:
            P.op(eng, lambda e: e.tensor_scalar(out=out.ap, in0=in0.ap, scalar1=apof(s1), scalar2=apof(s2),
                                                 op0=op0, op1=op1), [in0, s1, s2], [out], cost=vcost(eng, out, [in0], True))

    def cp(eng, out, in_):
        if eng == "act":
            P.op("act", lambda e: e.copy(out=out.ap, in_=in_.ap), [in_], [out], cost=(224.0 + fsz(out)) / 1.2)
        else:
            P.op(eng, lambda e: e.tensor_copy(out=out.ap, in_=in_.ap), [in_], [out], cost=vcost(eng, out, [in_], True))

    def memset(eng, out, val):
        P.op(eng, lambda e: e.memset(out.ap, val), [], [out])

    def fence(eng, old_bufs, new_bufs):
        P.op(eng, lambda e: e.memset(STAT.ap[:, 15:16], 0.0), [], list(old_bufs) + list(new_bufs))

    def chk(k, dumps):
        if stop != k:
            return
        r = 0
        for v in dumps:
            n = v.ap.shape[-1]
            P.dma("sp", out_d[r:r + 128, 0:n], v, key="st0", is_output=True)
            r += 128
        P.emit()
        raise _Stop()

    P.dma("sp", CST, cst_d, key="c0")
    P.dma("sp", COL, col_d, key="c1")
    P.dma("sp", ROW, row_d, key="c2")
    P.dma("pool", BADA, bada_d, key="c3")
    P.dma("pool", BINT, bint_d, key="c4")
    xkeys = ["x0", "x1", "x2", "x3"]
    for s_ in range(NSUB):
        P.dma("sp", XS[s_], x_d[s_ * 128:(s_ + 1) * 128, :], key=xkeys[s_])

    wada_v = wada_d.rearrange("(k p) n -> p k n", p=128)
    win_v = win_d.rearrange("(k p) n -> p k n", p=128)
    wout_v = wout_d.rearrange("(k p) n -> p k n", p=128)
    wup_v = wup_d.rearrange("(k p) n -> p k n", p=128)
    wdn_v = wdn_d.rearrange("(k p) n -> p k n", p=128)

    cp("dve", IDB, IDENT)
    memset("dve", ONESB, 1.0)
    ts("dve", NEGB, bfm, -1.0, None, ALU.mult)
    memset("dve", STATE, 0.0)
    sigm(CACT[:, 0:8], c_col)
    tt("dve", CACT[:, 8:16], CACT[:, 0:8], c_col, ALU.mult)
    for k in range(8):
        ts("dve", CB[:, k, :], ONES, CACT[:, 8 + k:9 + k], None, ALU.mult)
    ones_row = ONES[0:1, :]
    onesb_row = ONESB[0:1, :]

    def bcast_row(dst, row_slice, ps):
        mm(ps, ones_row, row_slice, True, True)
        cp("act", dst, ps)

    bcast_row(G1[:, 0:512], ROW[:, 0:512], PS_TMA)
    bcast_row(G1[:, 512:1024], ROW[:, 512:1024], PS_TMB)
    bcast_row(GT1[:, 0:512], ROW[:, 1024:1536], PS_TMA)
    bcast_row(GT1[:, 512:1024], ROW[:, 1536:2048], PS_TMB)
    bcast_row(GOUT, ROW[:, 5120:5632], PS_TMA)
    tt("dve", LBT, ROW[:, 4096:4608], ROW[:, 4608:5120], ALU.subtract)
    act(LBT, LBT, AF.Exp, scale=1.0)
    act(LBT, LBT, AF.Ln, bias=1.0, scale=1.0)
    act(LBT, LBT, AF.Exp, scale=-1.0)
    bcast_row(OMLB, LBT, PS_TMB)
    for c in range(4):
        for j in range(31):
            ts("dve", sub(DG, (slice(None), c * 31 + j, slice(None)), c), IDENT,
               wdwT[:, c * 31 + j:c * 31 + j + 1], None, ALU.mult)

    PS_CVA_full = PS_CT
    mod_ps = [PS_TMA, PS_TMB]
    wcount = [0]

    def mod_chunk(n, consume):
        wa = WA[wcount[0] % 3]
        ps = mod_ps[wcount[0] % 2]
        P.dma("pool", wa, wada_v[:, :, n * 512:(n + 1) * 512], key="wa%d" % (wcount[0] % 3))
        wcount[0] += 1
        mm(ps, onesb_row, BADA[:, n * 512:(n + 1) * 512], True, False)
        for k in range(8):
            mm(ps, CB[:, k, :], wa[:, k, :], False, k == 7)
        consume(ps)

    def half(vv, n):
        return vv[:, (n % 2) * 512:(n % 2) * 512 + 512]

    for n in (2, 3):
        mod_chunk(n, lambda ps, n=n: stt("dve", half(G1, n), ps, 1.0, half(G1, n), ALU.add, ALU.mult))
    for n in (0, 1):
        mod_chunk(n, lambda ps, n=n: cp("act", half(SH1, n), ps))
    for j in range(6):
        P.dma("pool", sub(WIN, (slice(None), slice(None), slice(j * 512, (j + 1) * 512)), j),
              win_v[:, :, j * 512:(j + 1) * 512], key="win%d" % j)
    for n in (4, 5):
        mod_chunk(n, lambda ps, n=n: tt("dve", half(GT1, n), ps, half(GT1, n), ALU.mult))
    for j in range(2):
        P.dma("pool", sub(WOUT, (slice(None), slice(None), slice(j * 512, (j + 1) * 512)), j),
              wout_v[:, :, j * 512:(j + 1) * 512], key="wout%d" % j)
    for n in (8, 9):
        bcast_row(HF[:, 0:512], ROW[:, 2048 + (n % 2) * 512:2048 + (n % 2) * 512 + 512], PS_CVA_full)
        mod_chunk(n, lambda ps, n=n: stt("dve", HF[:, 0:512], ps, 1.0, HF[:, 0:512], ALU.add, ALU.mult))
        P.dma("sp", V(modf_d[:, (n % 2) * 512:(n % 2) * 512 + 512], [modf_buf]), HF[:, 0:512], key="mf")
    for n in (6, 7):
        mod_chunk(n, lambda ps, n=n: cp("act", HF[:, 0:512], ps))
        P.dma("sp", V(modf_d[:, 1024 + (n % 2) * 512:1024 + (n % 2) * 512 + 512], [modf_buf]), HF[:, 0:512], key="mf")
    for n in (10, 11):
        bcast_row(HF[:, 0:512], ROW[:, 3072 + (n % 2) * 512:3072 + (n % 2) * 512 + 512], PS_CVA_full)
        mod_chunk(n, lambda ps, n=n: tt("dve", HF[:, 0:512], ps, HF[:, 0:512], ALU.mult))
        P.dma("sp", V(modf_d[:, 2048 + (n % 2) * 512:2048 + (n % 2) * 512 + 512], [modf_buf]), HF[:, 0:512], key="mf")

    chk(1, [G1, SH1, GT1, OMLB])
    fence("dve", R_st.bufs, R_ph1t.bufs)
    memset("dve", UE[0], 0.0)
    memset("dve", ATM, 0.0)
    memset("dve", ONES512, 1.0)
    memset("dve", NEGONES512, -1.0)

    def norm_a(X, Gt, SHt):
        ms = STAT[:, 0:1]
        rs = STAT[:, 1:2]
        act(HB, X, AF.Square, scale=1.0 / 32.0, accum=ms)
        rsqrt_small(rs, ms, RMS_EPS)
        stt("dve", HF, X, rs, Gt, ALU.mult, ALU.mult)
        tt("dve", HB, HF, SHt, ALU.add)

    def norm_b(hT, s_):
        for k in range(8):
            tr(PS_TRB[:, k * 128:(k + 1) * 128], HB[:, k * 128:(k + 1) * 128], IDB)
        dst = V(hT.ap[:, :, s_ * 128:(s_ + 1) * 128], [hT.bufs[s_]])
        src = V(PS_TRB.ap.rearrange("p (k t) -> p k t", k=8), PS_TRB.bufs)
        cp("act", dst, src)

    def norm_stage(X, Gt, SHt, hT, s_):
        norm_a(X, Gt, SHt)
        norm_b(hT, s_)

    def post_stage(Y, X, GTt, hf=None, hb=None, st=None):
        hf = HF if hf is None else hf
        hb = HB if hb is None else hb
        st = STAT if st is None else st
        ms = st[:, 2:3]
        rs = st[:, 3:4]
        if hb is PS_TR:
            act(hb, Y[:, 0:512], AF.Square, scale=1.0 / 32.0, accum=st[:, 4:5])
            act(hb, Y[:, 512:1024], AF.Square, scale=1.0 / 32.0, accum=st[:, 5:6])
            tt("dve", ms, st[:, 4:5], st[:, 5:6], ALU.add)
        else:
            act(hb, Y, AF.Square, scale=1.0 / 32.0, accum=ms)
        rsqrt_small(rs, ms, RMS_EPS)
        if hf is Y:
            for h_ in range(2):
                stt("dve", Y[:, h_ * 512:(h_ + 1) * 512], Y[:, h_ * 512:(h_ + 1) * 512], rs, GTt[:, h_ * 512:(h_ + 1) * 512],
                    ALU.mult, ALU.mult)
        else:
            stt("dve", hf, Y, rs, GTt, ALU.mult, ALU.mult)
        tt("dve", X, X, hf, ALU.add)

    SMv = SMALL
    NEGMID, EM, EL, SS4, RS4 = (SMv[:, 0:4], SMv[:, 4:8], SMv[:, 8:12], SMv[:, 12:16], SMv[:, 16:20])

    def xs_of(m):
        return [XS[(m % 2) * NSUB + s_] for s_ in range(NSUB)]

    def steps_A(m):
        st = []
        HT = HT2[m % 2]

        def pre():
            if m >= 1:
                for s_ in range(NSUB):
                    slot = (m % 2) * NSUB + s_
                    r0 = m * T + s_ * 128
                    P.dma("sp", XS[slot], x_d[r0:r0 + 128, :], key=xkeys[slot])
        st.append(pre)
        for s_ in range(NSUB):
            st.append(lambda s_=s_: norm_a(xs_of(m)[s_], G1, SH1))
            st.append(lambda s_=s_: norm_b(HT, s_))
        return st

    def steps_B(m):
        st = []
        HT = HT2[m % 2]
        QS = QS2[m % 2]
        ue = UE[m % 2]
        ue_next = UE[(m + 1) % 2]
        for c in range(4):
            def g1(c=c):
                wg = sub(WIN, (slice(None), slice(None), slice(512 + c * 128, 512 + (c + 1) * 128)), 1)
                for k in range(8):
                    mm(PS_FMA, wg[:, k, :], HT[:, k, :], k == 0, k == 7)
                sigm(TA[c % 2], PS_FMA, negbias=NEGB[:, 4 + c:5 + c], pool=True)

            def g2(c=c):
                wv = sub(WIN, (slice(None), slice(None), slice(c * 128, (c + 1) * 128)), 0)
                for k in range(8):
                    mm(PS_FMB, wv[:, k, :], HT[:, k, :], k == 0, k == 7)
                stt("dve", sub(ue, (slice(None), c, slice(30, 30 + T)), c), PS_FMB, bfm[:, c:c + 1], TA[c % 2], ALU.add, ALU.mult)
                cp("pool", sub(ue_next, (slice(None), c, slice(0, 30)), c), sub(ue, (slice(None), c, slice(T, T + 30)), c))
            st.append(g1)
            st.append(g2)
        for hd in range(4):
            def gq(hd=hd):
                wq = sub(WIN, (slice(None), slice(None), slice(1024 + hd * 128, 1024 + (hd + 1) * 128)), 2)
                ps = PS_FMA if hd % 2 == 0 else PS_FMB
                for k in range(8):
                    mm(ps, wq[:, k, :], HT[:, k, :], k == 0, k == 7)
                ta = TA[hd % 2]
                sigm(ta, ps, negbias=NEGB[:, 8 + hd:9 + hd], pool=True)
                stt("dve", sub(QS, (slice(None), hd, slice(None)), hd), ps, bfm[:, 8 + hd:9 + hd], ta, ALU.add, ALU.mult)
            st.append(gq)
        return st

    def steps_C(m):
        st = []
        ue = UE[m % 2]
        MIXT = MIXT2[m % 2]
        for c in range(4):
            pcv = PS_CVA if c % 2 == 0 else PS_CVB

            def cv(c=c, pcv=pcv, lo=0, hi=31):
                for j in range(lo, hi):
                    mm(pcv, sub(DG, (slice(None), c * 31 + j, slice(None)), c), sub(ue, (slice(None), c, slice(j, j + T)), c),
                       j == 0, j == 30)
            st.append(lambda c=c, pcv=pcv: cv(c, pcv, 0, 16))
            st.append(lambda c=c, pcv=pcv: cv(c, pcv, 16, 31))

            def gn0(c=c, pcv=pcv):
                act(YSB, pcv, AF.Identity, bias=bdw[:, c:c + 1], scale=1.0)
                act(Y2SB, pcv, AF.Square, bias=bdw[:, c:c + 1], scale=1.0)
            st.append(gn0)

            def gn(c=c, pcv=pcv):
                mm(PS_GNM, GM, YSB, True, True)
                mm(PS_GNE, GM, Y2SB, True, True)
                act(M2, PS_GNM, AF.Square)
                tt("dve", VAR, PS_GNE, M2, ALU.subtract)
                rsqrt_small(VAR, VAR, GN_EPS)
                tt("dve", DD, YSB, PS_GNM, ALU.subtract)
                tt("dve", DD, DD, VAR, ALU.mult)
                ts("dve", DD, DD, gng[:, c:c + 1], gnb[:, c:c + 1], ALU.mult, ALU.add)
                sigm(M2, DD, pool=True)
                tt("dve", sub(MIXT, (slice(None), c, slice(None)), c), DD, M2, ALU.mult)
            st.append(gn)
        return st

    def steps_D(m):
        st = []
        HT = HT2[m % 2]
        QS = QS2[m % 2]
        MIXT = MIXT2[m % 2]
        cf3 = V(PS_CF.ap.rearrange("p (h t) -> p h t", h=4), PS_CF.bufs)
        ct3 = V(PS_CT.ap.rearrange("p (h v) -> p h v", h=4), PS_CT.bufs)
        mk3 = V(MASK4.ap.rearrange("p (h t) -> p h t", h=4), MASK4.bufs)
        for s_ in range(NSUB):
            hts = V(HT.ap[:, :, s_ * 128:(s_ + 1) * 128], [HT.bufs[s_]])

            def tok_proj(ps, grp, hts=hts):
                w = sub(WIN, (slice(None), slice(None), slice(1536 + grp * 512, 2048 + grp * 512)), 3 + grp)
                mm(ps, onesb_row, BINT[:, grp * 512:(grp + 1) * 512], True, False)
                for k in range(8):
                    mm(ps, hts[:, k, :], w[:, k, :], False, k == 7)

            def d1(tok_proj=tok_proj):
                tok_proj(PS_TMA, 0)
                sigm(A1, PS_TMA, pos=False)
                tt("dve", KK, A1, OMLB, ALU.mult)
                act(LF, KK, AF.Ln, bias=1.0, scale=-1.0)

            def d2(tok_proj=tok_proj):
                tok_proj(PS_TMB, 1)
                cp("act", VB, PS_TMB)

            def d2b(tok_proj=tok_proj):
                tok_proj(PS_TMA, 2)
                sigm(GF, PS_TMA, pool=True)
                tt("dve", GF, PS_TMA, GF, ALU.mult)
                tt("dve", GF, GF, GOUT, ALU.mult)

            def d3():
                mm(PS_CT, CC, LF, True, True)
                for hd in range(4):
                    mm(PS_CF[:, hd * 128:(hd + 1) * 128], LF[:, hd * 128:(hd + 1) * 128], TRI, True, True)
                act(A1, PS_CT, AF.Exp, scale=-1.0)
                tt("dve", KT, KK, A1, ALU.mult)
                ts("dve", NEGMID, cf3[:, :, 63], -1.0, None, ALU.mult)
                act(EM, cf3[:, :, 63], AF.Exp, scale=1.0)
                act(EL, cf3[:, :, 127], AF.Exp, scale=1.0)
                for hd in range(4):
                    act(EP[:, hd, :], cf3[:, hd, :], AF.Exp, bias=NEGMID[:, hd:hd + 1], scale=1.0)

            def d4(s_=s_):
                for hd in range(4):
                    tr(PS_CTB[:, hd * 128:(hd + 1) * 128], KT[:, hd * 128:(hd + 1) * 128], IDB)
                cp("act", KTT, V(PS_CTB.ap[:, 0:512].rearrange("p (h t) -> p h t", h=4), PS_CTB.bufs))
                qs_s = V(QS.ap[:, :, s_ * 128:(s_ + 1) * 128], QS.bufs)
                tt("dve", QT, qs_s, EP, ALU.mult)
                for hd in range(4):
                    ts("dve", SM[:, hd, :], STATE[:, hd, :], EM[:, hd:hd + 1], None, ALU.mult)

            def d5():
                for hd in range(4):
                    mm(PS_CF[0:64, hd * 128:hd * 128 + 64], KTT[:, hd, 0:64], QT[:, hd, 0:64], True, True)
                    mm(PS_CF[:, hd * 128 + 64:hd * 128 + 128], KTT[:, hd, :], QT[:, hd, 64:128], True, True)
                tt("dve", ATM[0:64, :, 0:64], cf3[0:64, :, 0:64], mk3[0:64, :, 0:64], ALU.mult)
                tt("dve", ATM[:, :, 64:128], cf3[:, :, 64:128], mk3[:, :, 64:128], ALU.mult)
                for hd in range(4):
                    mm(PS_CT[:, hd * 128:(hd + 1) * 128], KT[:, hd * 128:(hd + 1) * 128], VB[:, hd * 128:(hd + 1) * 128], True, True)
                for hd in range(4):
                    act(TMPS[:, hd, :], ct3[:, hd, :], AF.Copy, scale=EP[:, hd, 127:128])

            def d6():
                for hd in range(4):
                    o_hd = PS_O[:, hd * 128:(hd + 1) * 128]
                    mm(o_hd, ATM[:, hd, :], VB[:, hd * 128:(hd + 1) * 128], True, False)
                    mm(o_hd, QT[:, hd, :], SM[:, hd, :], False, True)
                for hd in range(4):
                    act(HG[:, hd * 128:(hd + 1) * 128], PS_O[:, hd * 128:(hd + 1) * 128], AF.Square,
                        scale=float(1.0 / np.sqrt(128.0)), accum=SS4[:, hd:hd + 1])
                rsqrt_small(RS4, SS4, RMS_EPS)
                for hd in range(4):
                    stt("dve", HG[:, hd * 128:(hd + 1) * 128], PS_O[:, hd * 128:(hd + 1) * 128], RS4[:, hd:hd + 1],
                        GF[:, hd * 128:(hd + 1) * 128], ALU.mult, ALU.mult)
                for hd in range(4):
                    stt("dve", STATE[:, hd, :], STATE[:, hd, :], EL[:, hd:hd + 1], TMPS[:, hd, :], ALU.mult, ALU.add)

            def d7(s_=s_):
                for hd in range(4):
                    tr(PS_TMBB[:, hd * 128:(hd + 1) * 128], HG[:, hd * 128:(hd + 1) * 128], IDB)
                dst = V(MIXT.ap[:, 4:8, s_ * 128:(s_ + 1) * 128], [MIXT.bufs[4 + s_]])
                cp("act", dst, V(PS_TMBB.ap[:, 0:512].rearrange("p (h t) -> p h t", h=4), PS_TMBB.bufs))
            st.extend([d1, d2, d2b, d3, d4, d5, d6, d7])
        return st

    def steps_E(m):
        st = []
        MIXT = MIXT2[m % 2]
        for s_ in range(NSUB):
            lh = V(MIXT.ap[:, :, s_ * 128:(s_ + 1) * 128], list(MIXT.bufs[0:4]) + [MIXT.bufs[4 + s_]])

            def e1(lh=lh, hf=0):
                wo = sub(WOUT, (slice(None), slice(None), slice(hf * 512, (hf + 1) * 512)), hf)
                for k in range(8):
                    mm(PS_Y[:, hf * 512:(hf + 1) * 512], lh[:, k, :], wo[:, k, :], k == 0, k == 7)

            def e2(s_=s_):
                post_stage(PS_Y, xs_of(m)[s_], GT1, HFP, HBP, STAT2)
                ti = m * NSUB + s_
                P.dma("sp", V(x1_d[ti * 128:(ti + 1) * 128, :], [x1_bufs[ti]]), xs_of(m)[s_], key="st%d" % (ti % 4))
            st.append(lambda lh=lh: e1(lh, 0))
            st.append(lambda lh=lh: e1(lh, 1))
            st.append(e2)
        return st

    def run_interleaved(xs_, ys_):
        nx, ny = len(xs_), len(ys_)
        ix = iy = 0
        while ix < nx or iy < ny:
            if iy >= ny or (ix < nx and ix * max(ny, 1) <= iy * max(nx, 1)):
                xs_[ix]()
                ix += 1
            else:
                ys_[iy]()
                iy += 1

    for f_ in steps_A(0) + steps_B(0) + steps_C(0):
        f_()
    for m in range(NM):
        xstream = steps_D(m) + steps_E(m)
        ystream = (steps_A(m + 1) + steps_B(m + 1) + steps_C(m + 1)) if m + 1 < NM else []
        if PH1_MODEL:
            P.interleave([xstream, ystream])
        else:
            run_interleaved(xstream, ystream)
        chk(7 + 100 * m, [xs_of(m)[0], xs_of(m)[1]])

    chk(6, [XS[0]])
    fence("dve", R_ph1w.bufs + R_ph1t.bufs, R_ph2.bufs)
    P.dma("sp", G2, V(modf_d[:, 0:1024], [modf_buf]), key="mf")
    P.dma("sp", SH2, V(modf_d[:, 1024:2048], [modf_buf]), key="mf")
    P.dma("sp", GT2, V(modf_d[:, 2048:3072], [modf_buf]), key="mf")
    for j in range(8):
        P.dma("pool", sub(WUP, (slice(None), slice(None), slice(j * 512, (j + 1) * 512)), j),
              wup_v[:, :, j * 512:(j + 1) * 512], key="wup%d" % j)
        P.dma("pool", sub(WDN, (slice(None), slice(4 * j, 4 * j + 4), slice(None)), j),
              wdn_v[:, 4 * j:4 * j + 4, :], key="wdn%d" % j)

    def load_x1(m):
        for s_ in range(NSUB):
            slot = (m % 2) * NSUB + s_
            ti = m * NSUB + s_
            P.dma("sp", XS[slot], V(x1_d[ti * 128:(ti + 1) * 128, :], [x1_bufs[ti]]), key=xkeys[slot])

    PS_UP = [pv(1), pv(2), pv(3)]
    PS_Y2 = [V(psum[:, 4 * 512:6 * 512], [bankbuf[4], bankbuf[5]]), V(psum[:, 6 * 512:8 * 512], [bankbuf[6], bankbuf[7]])]

    def steps_N2(m):
        st = [lambda: load_x1(m)]
        for s_ in range(NSUB):
            st.append(lambda s_=s_: norm_a(xs_of(m)[s_], G2, SH2))
            st.append(lambda s_=s_: norm_b(H2T2[m % 2], s_))
        return st

    def steps_UP(m):
        st = []
        H2T = H2T2[m % 2]
        for fp in range(16):
            def up(fp=fp):
                ps = PS_UP[fp % 3]
                for h_ in range(2):
                    f = 2 * fp + h_
                    wu = sub(WUP, (slice(None), slice(None), slice(f * 128, (f + 1) * 128)), f // 4)
                    for k in range(8):
                        mm(ps[:, h_ * T:(h_ + 1) * T], wu[:, k, :], H2T[:, k, :], k == 0, k == 7)
                rr = RR[fp % 2]
                act(rr, ps, AF.Relu)
                dst = V(AT.ap[:, 2 * fp:2 * fp + 2, :], [AT.bufs[2 * fp], AT.bufs[2 * fp + 1]])
                rr3 = V(rr.ap.rearrange("p (f t) -> p f t", f=2), rr.bufs)
                tt("pool", dst, rr3, rr3, ALU.mult)
            st.append(up)
        return st

    def steps_DOWN(m):
        st = []
        for fp in range(16):
            def dn(fp=fp):
                for h_ in range(2):
                    f = 2 * fp + h_
                    for s_ in range(NSUB):
                        for hf in range(2):
                            wd = sub(WDN, (slice(None), f, slice(hf * 512, (hf + 1) * 512)), f // 4)
                            lh = V(AT.ap[:, f, s_ * 128:(s_ + 1) * 128], [AT.bufs[f]])
                            mm(PS_Y2[s_][:, hf * 512:(hf + 1) * 512], lh, wd, f == 0, f == 31)
            st.append(dn)
        for s_ in range(NSUB):
            def po(s_=s_):
                post_stage(PS_Y2[s_], xs_of(m)[s_], GT2, PS_Y2[s_], PS_TR, STAT3)
                ti = m * NSUB + s_
                P.dma("sp", out_d[ti * 128:(ti + 1) * 128, :], xs_of(m)[s_], key="st%d" % (ti % 4), is_output=True)
            st.append(po)
        return st

    for f_ in steps_N2(0):
        f_()
    for m in range(NM):
        streams = [steps_DOWN(m), steps_UP(m)]
        if m + 1 < NM:
            streams.append(steps_N2(m + 1))
        P.interleave(streams, ready=lambda i, pos: not (i == 0 and pos[0] < 16 and pos[1] <= pos[0]))

    P.emit()
    return nc, P


def _consts():
    s = np.arange(128)[:, None]
    t = np.arange(128)[None, :]
    ident = (s == t).astype(np.float32)
    tri = (s <= t).astype(np.float32)
    cc = np.zeros((128, 128), np.float32)
    cc[(s >= 64) & (s <= t)] = 1.0
    cc[(s <= 63) & (s > t)] = -1.0
    gm = ((s // 64) == (t // 64)).astype(np.float32) / 64.0
    ones = np.ones((128, 128), np.float32)
    mask4 = np.tile(tri, (1, 4))
    return np.ascontiguousarray(np.concatenate([ident, tri, cc, gm, ones, mask4], axis=1))


_NC_CACHE = {}


def run(inputs, S=None, stop=0):
    x = np.asarray(inputs["x"], np.float32)
    B, S_, _ = x.shape
    S = S_ if S is None else S
    f = lambda k: np.asarray(inputs[k], np.float32)
    c = f("c")
    b_in = f("b_in")[0]
    w_dw = f("w_dw")[0]
    lbl = f("lb_logits")
    rowpack = np.concatenate([f("g_pre_mix")[0], f("g_post_mix")[0], f("g_pre_ffn")[0], f("g_post_ffn")[0],
                              lbl[0], lbl[1], f("g_hgrn_out")[0]])[None, :]
    col_common = np.concatenate([
        b_in[:1536].reshape(12, 128).T,
        w_dw.T.reshape(4, 128, 31).transpose(1, 0, 2).reshape(128, 124),
        f("b_dw")[0].reshape(4, 128).T, f("gn_gain")[0].reshape(4, 128).T, f("gn_bias")[0].reshape(4, 128).T], axis=1)
    shared = {
        "w_ada": np.ascontiguousarray(f("w_ada")[0]), "w_in": np.ascontiguousarray(f("w_in")[0]),
        "w_out": np.ascontiguousarray(f("w_out")[0]), "w_up": np.ascontiguousarray(f("w_up")[0]),
        "w_down": np.ascontiguousarray(f("w_down")[0]),
        "rowpack": np.ascontiguousarray(rowpack), "b_ada_row": np.ascontiguousarray(f("b_ada")),
        "b_in_tok": np.ascontiguousarray(b_in[None, 1536:]), "consts": _consts(),
    }
    in_maps = []
    for b in range(B):
        m = dict(shared)
        m["x"] = np.ascontiguousarray(x[b])
        m["colpack"] = np.ascontiguousarray(np.concatenate([c[b].reshape(8, 128).T, col_common], axis=1))
        in_maps.append(m)
    if (S, stop) not in _NC_CACHE:
        _NC_CACHE[(S, stop)] = build_nc(S, stop)[0]
    res = run_bass_kernel_spmd(_NC_CACHE[(S, stop)], in_maps, core_ids=list(range(B)))
    return np.stack([np.asarray(r["out"], np.float32) for r in res.results], axis=0)


def kernel(**inputs):
    return run(inputs)
```
